# Optimizing a Trainium2 kernel written in Bass

```python
import jax, jax.numpy as jnp
from jax import lax
import numpy as np

D_MODEL = 1024
BATCH = 8
SEQ = 4096
DEPTH = 4
DEC_BATCH = 1
DEC_SEQ = 16384
PAST_LEN = 128

N_MIXERS = 3
D_FF = 4 * D_MODEL
SHORT_CONV_W = 3
CONFORMER_CONV_W = 31
FNET_GROUPS = 4
FNET_GROUP_DIM = D_MODEL // FNET_GROUPS
N_ADA = 6
EPS = 1e-6
N_LAYERS_A = (DEPTH + 2) // 3
N_LAYERS_B = (DEPTH + 1) // 3
N_LAYERS_C = DEPTH // 3

kernel_name = "hybrid_conv_fourier_conformer_adaln_encoder"


def rmsnorm(x, g):
    xf = x.astype(jnp.float32)
    y = xf * lax.rsqrt(jnp.mean(xf * xf, axis=-1, keepdims=True) + EPS)
    return (y * g.astype(jnp.float32)).astype(x.dtype)


def layernorm(x, g, b):
    xf = x.astype(jnp.float32)
    mu = jnp.mean(xf, axis=-1, keepdims=True)
    xc = xf - mu
    y = xc * lax.rsqrt(jnp.mean(xc * xc, axis=-1, keepdims=True) + EPS)
    return (y * g.astype(jnp.float32) + b.astype(jnp.float32)).astype(x.dtype)


def depthwise_conv(x, w):
    k = w.shape[0]
    pad = k // 2
    return lax.conv_general_dilated(
        x, w[:, None, :].astype(x.dtype), window_strides=(1,), padding=((pad, pad),),
        dimension_numbers=("NWC", "WIO", "NWC"), feature_group_count=x.shape[-1])


def mixer_short_conv(h, w_in, conv_w, w_out):
    b_gate, c_gate, v = jnp.split(h @ w_in, 3, axis=-1)
    return (b_gate * depthwise_conv(c_gate * v, conv_w)) @ w_out


def mixer_fourier(h, w_out, b_out):
    bsz, s, d = h.shape
    hg = h.astype(jnp.float32).reshape(bsz, s, FNET_GROUPS, FNET_GROUP_DIM)
    f = jnp.fft.fft2(hg, axes=(1, 3), norm="ortho").real
    return f.reshape(bsz, s, d).astype(h.dtype) @ w_out + b_out


def mixer_conformer(h, w_pw1, b_pw1, dw_w, dw_b, ln_g, ln_b, w_pw2, b_pw2):
    a, g = jnp.split(h @ w_pw1 + b_pw1, 2, axis=-1)
    u = a * jax.nn.sigmoid(g)
    u = depthwise_conv(u, dw_w) + dw_b
    u = jax.nn.silu(layernorm(u, ln_g, ln_b))
    return u @ w_pw2 + b_pw2


def squared_relu_mlp(h, w_up, w_down):
    return jnp.square(jax.nn.relu(h @ w_up)) @ w_down


def trunk(x, c, ada_w, ada_b, norm_mix, norm_mlp,
          a_w_in, a_conv_w, a_w_out,
          b_w_out, b_b_out,
          c_w_pw1, c_b_pw1, c_dw_w, c_dw_b, c_ln_g, c_ln_b, c_w_pw2, c_b_pw2,
          mlp_w_up, mlp_w_down, final_norm):
    c_act = jax.nn.silu(c)
    for i in range(DEPTH):
        mod = (c_act @ ada_w[i] + ada_b[i])[:, None, :]
        sh1, sc1, g1, sh2, sc2, g2 = jnp.split(mod, N_ADA, axis=-1)
        h = rmsnorm(x, norm_mix[i]) * (1 + sc1) + sh1
        kind, j = i % N_MIXERS, i // N_MIXERS
        if kind == 0:
            m = mixer_short_conv(h, a_w_in[j], a_conv_w[j], a_w_out[j])
        elif kind == 1:
            m = mixer_fourier(h, b_w_out[j], b_b_out[j])
        else:
            m = mixer_conformer(h, c_w_pw1[j], c_b_pw1[j], c_dw_w[j], c_dw_b[j],
                                c_ln_g[j], c_ln_b[j], c_w_pw2[j], c_b_pw2[j])
        x = x + g1 * m
        h = rmsnorm(x, norm_mlp[i]) * (1 + sc2) + sh2
        x = x + g2 * squared_relu_mlp(h, mlp_w_up[i], mlp_w_down[i])
    return rmsnorm(x, final_norm)


def setup_inputs(seed: int = 0) -> dict:
    key = jax.random.key(seed)
    ks = jax.random.split(key, 32)
    f32 = jnp.float32
    D = D_MODEL

    def nrm(k, shape, scale):
        return jax.random.normal(k, shape, f32) * scale

    def gain(k, shape):
        return 1.0 + 0.02 * jax.random.normal(k, shape, f32)

    return {
        "x_prompt": nrm(ks[0], (BATCH, SEQ, D), 1.0),
        "x_sample": nrm(ks[1], (DEC_BATCH, DEC_SEQ, D), 1.0),
        "c_prompt": nrm(ks[2], (BATCH, D), 1.0),
        "c_sample": nrm(ks[3], (DEC_BATCH, D), 1.0),
        "ada_w": nrm(ks[4], (DEPTH, D, N_ADA * D), 0.5 * D ** -0.5),
        "ada_b": nrm(ks[5], (DEPTH, N_ADA * D), 0.02),
        "norm_mix": gain(ks[6], (DEPTH, D)),
        "norm_mlp": gain(ks[7], (DEPTH, D)),
        "a_w_in": nrm(ks[8], (N_LAYERS_A, D, 3 * D), D ** -0.5),
        "a_conv_w": nrm(ks[9], (N_LAYERS_A, SHORT_CONV_W, D), SHORT_CONV_W ** -0.5),
        "a_w_out": nrm(ks[10], (N_LAYERS_A, D, D), D ** -0.5),
        "b_w_out": nrm(ks[11], (N_LAYERS_B, D, D), D ** -0.5),
        "b_b_out": nrm(ks[12], (N_LAYERS_B, D), 0.02),
        "c_w_pw1": nrm(ks[13], (N_LAYERS_C, D, 2 * D), D ** -0.5),
        "c_b_pw1": nrm(ks[14], (N_LAYERS_C, 2 * D), 0.02),
        "c_dw_w": nrm(ks[15], (N_LAYERS_C, CONFORMER_CONV_W, D), CONFORMER_CONV_W ** -0.5),
        "c_dw_b": nrm(ks[16], (N_LAYERS_C, D), 0.02),
        "c_ln_g": gain(ks[17], (N_LAYERS_C, D)),
        "c_ln_b": nrm(ks[18], (N_LAYERS_C, D), 0.02),
        "c_w_pw2": nrm(ks[19], (N_LAYERS_C, D, D), D ** -0.5),
        "c_b_pw2": nrm(ks[20], (N_LAYERS_C, D), 0.02),
        "mlp_w_up": nrm(ks[21], (DEPTH, D, D_FF), D ** -0.5),
        "mlp_w_down": nrm(ks[22], (DEPTH, D_FF, D), D_FF ** -0.5),
        "final_norm": gain(ks[23], (D,)),
    }


def reference(x_prompt, x_sample, c_prompt, c_sample, ada_w, ada_b, norm_mix, norm_mlp,
              a_w_in, a_conv_w, a_w_out, b_w_out, b_b_out,
              c_w_pw1, c_b_pw1, c_dw_w, c_dw_b, c_ln_g, c_ln_b, c_w_pw2, c_b_pw2,
              mlp_w_up, mlp_w_down, final_norm):
    y_prompt = trunk(x_prompt, c_prompt, ada_w, ada_b, norm_mix, norm_mlp,
                     a_w_in, a_conv_w, a_w_out, b_w_out, b_b_out,
                     c_w_pw1, c_b_pw1, c_dw_w, c_dw_b, c_ln_g, c_ln_b, c_w_pw2, c_b_pw2,
                     mlp_w_up, mlp_w_down, final_norm)
    y_sample = trunk(x_sample, c_sample, ada_w, ada_b, norm_mix, norm_mlp,
                     a_w_in, a_conv_w, a_w_out, b_w_out, b_b_out,
                     c_w_pw1, c_b_pw1, c_dw_w, c_dw_b, c_ln_g, c_ln_b, c_w_pw2, c_b_pw2,
                     mlp_w_up, mlp_w_down, final_norm)
    return (y_prompt, y_sample)
```

```python
import numpy as np
from contextlib import ExitStack
import concourse.bass as bass
import concourse.mybir as mybir
from concourse.bass_utils import run_bass_kernel_spmd

F32 = mybir.dt.float32
BF16 = mybir.dt.bfloat16
I32 = mybir.dt.int32
AF = mybir.ActivationFunctionType
ALU = mybir.AluOpType

D = 1024
KC = 8
W = 1058
H = 17
T = 1024
NSEG = 6
NSEGA = 20
NKK = 18
FOFF = 111
NT = [(0, 353), (353, 706), (706, 1058)]
NT_TRIM = [(16, 358), (358, 700), (700, 1042)]
EPS = 1e-6
NS = 3
SLOT = 8192
SP = 4096
SS = 16384
WIN = 2048 + 2 * H

PP = {}
_o = 0
for _n, _c in [("ada_b", 4 * 48), ("nmix", 32), ("nmlp", 32), ("fin", 8), ("aconv", 2 * 3 * 8), ("bb", 8),
               ("cb1", 16), ("cdw", 31 * 8), ("cdb", 8), ("lng", 8), ("lnb", 8), ("cb2", 8),
               ("mask", NSEGA * 2 * H), ("eps", 1), ("zero", 1), ("c", 16)]:
    PP[_n] = _o
    _o += _c
NPP = _o


class Prog:
    ENG = ("pe", "act", "dve", "pool", "sp")
    SELF_SYNC = {"pe": False, "act": True, "dve": True, "pool": True, "sp": False}

    def __init__(self):
        self.q = {e: [] for e in self.ENG}
        self.cnt = {}
        self.lastw = {}
        self.readers = {}
        self.waited = {e: {} for e in self.ENG}
        self.sems = {}
        self.semnames = set("E_" + e for e in self.ENG)

    def _deps(self, eng, reads, writes):
        d = {}

        def add(s, v):
            if s == "E_" + eng and not self.SELF_SYNC[eng]:
                return
            d[s] = max(d.get(s, 0), v)
        for r in reads:
            for s, v in self.lastw.get(r, {}).items():
                add(s, v)
        for w in writes:
            for s, v in self.lastw.get(w, {}).items():
                add(s, v)
            for s, v in self.readers.get(w, {}).items():
                add(s, v)
        return d

    def _waits(self, eng, d):
        for s, v in d.items():
            if self.waited[eng].get(s, 0) >= v:
                continue
            self.waited[eng][s] = v
            self.q[eng].append(lambda e, s=s, v=v: e.wait_ge(self.sems[s], v))

    def _record(self, ev, reads, writes):
        s, v = ev
        for r in reads:
            rr = self.readers.setdefault(r, {})
            rr[s] = max(rr.get(s, 0), v)
        for w in writes:
            ww = self.lastw.setdefault(w, {})
            ww[s] = max(ww.get(s, 0), v)
            self.readers[w] = {}

    def ops(self, eng, fns, reads=(), writes=()):
        self._waits(eng, self._deps(eng, reads, writes))
        s = "E_" + eng
        self.cnt[s] = self.cnt.get(s, 0) + 1
        v = self.cnt[s]
        for fn in fns[:-1]:
            self.q[eng].append(lambda e, fn=fn: fn(e))
        last = fns[-1]
        self.q[eng].append(lambda e, fn=last, s=s: fn(e).then_inc(self.sems[s], 1))
        self._record((s, v), reads, writes)

    def op(self, eng, fn, reads=(), writes=()):
        self.ops(eng, [fn], reads, writes)

    def ev(self, qeng, sem, inc, fn, reads=(), writes=()):
        self.semnames.add(sem)
        self._waits(qeng, self._deps(qeng, reads, writes))
        self.cnt[sem] = self.cnt.get(sem, 0) + inc
        v = self.cnt[sem]
        self.q[qeng].append(lambda e, fn=fn, sem=sem, inc=inc: fn(e).then_inc(self.sems[sem], inc))
        self._record((sem, v), reads, writes)

    def dma(self, qeng, sem, out, in_, reads=(), writes=()):
        self.ev(qeng, sem, 16, lambda e, out=out, in_=in_: e.dma_start(out=out, in_=in_), reads, writes)

    def final_wait(self, eng):
        for s, v in self.cnt.items():
            if s.startswith("E_"):
                continue
            if self.waited[eng].get(s, 0) >= v:
                continue
            self.waited[eng][s] = v
            self.q[eng].append(lambda e, s=s, v=v: e.wait_ge(self.sems[s], v))


def build_program():
    nc = bass.Bass("TRN2", target_bir_lowering=False)
    P = Prog()

    def din(name, shape, dt=F32):
        return nc.dram_tensor(name, list(shape), dt, kind="ExternalInput")

    xs = din("xs", [NSEGA, D, W])
    ppd = din("pp", [128, NPP])
    ada_w = din("ada_w", [4, D, 6 * D])
    a_w_in = din("a_w_in", [2, D, 3 * D])
    a_w_out = din("a_w_out", [2, D, D])
    b_w_out = din("b_w_out", [D, D])
    c_w_pw1 = din("c_w_pw1", [D, 2 * D])
    c_w_pw2 = din("c_w_pw2", [D, D])
    w_up = din("mlp_w_up", [4, D, 4 * D])
    w_down = din("mlp_w_down", [4, 4 * D, D])
    tab_cs = din("tab_cs", [256, 512])
    tab_w1 = din("tab_w1", [128, 512])
    tab_e2 = [None, din("tab_e2p", [128, 128, 2, 128])]
    tab_w1s = din("tab_w1s", [128, 512])
    tab_e2w = din("tab_e2w", [128, 128, 2 * NKK])
    tab_id = din("tab_id", [128, 128])
    yout = nc.dram_tensor("y", [D, NSEG * T], F32, kind="ExternalOutput")

    xd = nc.dram_tensor("xd", [NSEG, D, W], F32)
    yzp = nc.dram_tensor("yzp", [SP, 8, 256], BF16)
    yzsd = nc.dram_tensor("yzsd", [8, SS, 256], BF16)
    fpd = nc.dram_tensor("fpd", [8, 128, SP + 2 * H], BF16)
    fsd = nc.dram_tensor("fsd", [8, 128, NKK * 128], BF16)

    off = [16512]

    def alloc(name, shape, dt, at=None):
        nbytes = int(np.prod(shape[1:])) * (4 if dt in (F32, I32) else 2)
        nbytes = (nbytes + 31) // 32 * 32
        if at is None:
            at = off[0]
            off[0] += nbytes
        assert at + nbytes <= 229344, (name, at, nbytes)
        return nc.alloc_sbuf_tensor_at(name, list(shape), dt, offset=at)

    wsl = [alloc("ws%d" % i, [128, SLOT], BF16) for i in range(NS)]
    pp = alloc("pp", [128, NPP], F32)
    modt = alloc("modt", [128, 4, 48, 2], F32)
    drv = alloc("drv", [128, 4, 2, 8, 2], F32)
    gbt = alloc("gbt", [128, 2, 8, 2], F32)
    cact = alloc("cact", [128, 8, 2], BF16)
    csil = alloc("csil", [128, 8, 2], F32)
    ones = alloc("ones", [128, 128], BF16)
    ident = alloc("ident", [128, 128], BF16)
    cs_t = alloc("cs_t", [128, 2, 512], BF16)
    w1_t = alloc("w1_t", [128, 2, 256], BF16)
    base_ph = off[0]
    xt = alloc("xt", [128, 8, W], F32)
    ht = alloc("ht", [128, 8, W], BF16)
    hid_off = off[0]
    hid = alloc("hid", [128, 32, W], BF16)
    CH = W * 2
    vbuf = alloc("vbuf", [128, 8, W], F32, at=hid_off + 8 * CH)
    yzs = [alloc("yzs%d" % i, [128, 8, 2, 128], BF16, at=hid_off + 8 * i * CH) for i in range(4)]
    scr = [alloc("scr%d" % i, [128, W], F32) for i in range(4)]
    dg0_off = off[0]
    dg = [alloc("dg%d" % i, [128, 31, 128], BF16) for i in range(2)]
    dg1_off = dg0_off + 31 * 128 * 2
    Yt = alloc("Yt", [128, 128, 2, 128], BF16, at=base_ph)
    At = alloc("At", [128, 128, 256], BF16, at=base_ph + 65536)
    fTs = alloc("fTs", [128, 1, SS + 2 * H], BF16, at=base_ph)
    fTp = alloc("fTp", [128, 4, SP + 2 * H], BF16, at=base_ph)
    e2w = alloc("e2w", [128, 128, 2 * NKK], BF16, at=base_ph + 131072)
    fwc = [alloc("fwc%d" % i, [128, NKK * 128], BF16, at=base_ph + 140288 + 4608 * i) for i in range(2)]
    w1s_t = alloc("w1s_t", [128, 2, 256], BF16, at=base_ph + 149504)
    ps = nc.alloc_psum_tensor("ps", [128, 8, 512], F32)

    def ppc(name, i=0, n=1):
        o = PP[name] + i
        return pp[:, o:o + n]

    panels = []

    def k1024(w2d, c0):
        return (w2d[:, c0:c0 + 1024].rearrange("(k p) n -> p k n", p=128), (8, 1024))

    def layer_panels(i):
        kind, j = i % 3, i // 3
        pl = []
        if kind == 0:
            w = a_w_in.ap()[j]
            pl += [("win_c", k1024(w, 1024)), ("win_v", k1024(w, 2048)), ("win_b", k1024(w, 0)),
                   ("wout", k1024(a_w_out.ap()[j], 0))]
        elif kind == 1:
            pl += [("wout", k1024(b_w_out.ap(), 0))]
        else:
            pl += [("pw1a", k1024(c_w_pw1.ap(), 0)), ("pw1g", k1024(c_w_pw1.ap(), 1024)),
                   ("wout", k1024(c_w_pw2.ap(), 0))]
        return pl

    def mlp_panels(i):
        pl = [("up%d" % p, k1024(w_up.ap()[i], 1024 * p)) for p in range(4)]
        pl += [("dn%d" % p, (w_down.ap()[i][:, 256 * p:256 * p + 256].rearrange("(k p) n -> p k n", p=128), (32, 256)))
               for p in range(4)]
        return pl

    def e2_panels(v):
        return [("e2_%d" % s, (tab_e2[v].ap()[:, 32 * s:32 * s + 32, :, :].rearrange("p k c n -> p k (c n)"), (32, 256)))
                for s in range(4)]

    def deferred_ada(seg):
        return (2 + (seg - 1) // 6, (seg - 1) % 6) if 1 <= seg <= 12 else None

    for i in range(2):
        panels += [("ada%d" % p, k1024(ada_w.ap()[i], 1024 * p)) for p in range(6)]
    for seg in range(NSEGA):
        if deferred_ada(seg):
            di, dp = deferred_ada(seg)
            panels += [("ada%d" % dp, k1024(ada_w.ap()[di], 1024 * dp))]
        panels += layer_panels(0) + mlp_panels(0)
    panels += e2_panels(1) + e2_panels(1)
    for seg in range(NSEG):
        panels += layer_panels(1) + mlp_panels(1) + layer_panels(2) + mlp_panels(2) + layer_panels(3) + mlp_panels(3)

    pst = {"i": 0, "loaded": 0}

    def next_panel(tag):
        i = pst["i"]
        pst["i"] += 1
        assert panels[i][0] == tag, (i, panels[i][0], tag)
        while pst["loaded"] < min(len(panels), i + NS):
            j = pst["loaded"]
            src, (a, b) = panels[j][1]
            sl = j % NS
            dst = wsl[sl][:, 0:a * b].rearrange("p (a b) -> p a b", a=a)
            P.dma("pool", "w%d" % sl, dst, src, reads=[], writes=[("ws", sl)])
            pst["loaded"] += 1
        sl = i % NS
        a, b = panels[i][1][1]
        return sl, wsl[sl][:, 0:a * b].rearrange("p (a b) -> p a b", a=a)

    gst = {"g": 0}
    cur = {"NT": NT}

    def mm_group(nw, pairs, reads):
        bank = gst["g"] % 8
        gst["g"] += 1
        n = len(pairs)
        fns = [lambda e, l=l, r=r, i=i: e.matmul(ps[:, bank, 0:nw], lhsT=l, rhs=r, start=(i == 0), stop=(i == n - 1))
               for i, (l, r) in enumerate(pairs)]
        P.ops("pe", fns, reads=reads, writes=[("ps", bank)])
        return bank

    def linear(tag, rhs, rhs_keys, evac, kc=8, mper=8, mcols=128, n_major=False, tile_hook=None, hook_delay=3):
        sl, wv = next_panel(tag)
        P.stage = tag
        rk = rhs_keys if callable(rhs_keys) else (lambda ni: rhs_keys)
        if tile_hook is not None:
            n_major = True
        NTc = cur["NT"]
        order = [(m, ni) for ni in range(len(NTc)) for m in range(mper)] if n_major else [(m, ni) for m in range(mper) for ni in range(len(NTc))]
        fire = {}
        if tile_hook is not None:
            for ni in range(len(NTc)):
                fire.setdefault(min((ni + 1) * mper - 1 + hook_delay, len(order) - 1), []).append(ni)
        for gi, (m, ni) in enumerate(order):
            n0, n1 = NTc[ni]
            pairs = [(wv[:, k, m * mcols:(m + 1) * mcols], rhs(k, n0, n1)) for k in range(kc)]
            bank = mm_group(n1 - n0, pairs, reads=[("ws", sl)] + rk(ni))
            evac(m, ni, n0, n1, bank)
            for nj in fire.get(gi, []):
                tile_hook(nj)
                P.stage = tag

    P.dma("sp", "d_pp", pp[:], ppd.ap(), writes=[("pp",)])
    P.dma("pool", "d_cs", cs_t[:], tab_cs.ap().rearrange("(k p) n -> p k n", p=128), writes=[("cs",)])
    P.dma("pool", "d_w1", w1_t[:], tab_w1.ap().rearrange("p (k n) -> p k n", k=2), writes=[("w1",)])
    P.dma("pool", "d_id", ident[:], tab_id.ap(), writes=[("ident",)])
    P.op("dve", lambda e: e.memset(ones[:], 1.0), writes=[("ones",)])
    cv = pp[:, PP["c"]:PP["c"] + 16].rearrange("p (k c) -> p k c", k=8)
    P.op("act", lambda e: e.activation(out=csil[:], in_=cv, func=AF.Silu), reads=[("pp",)], writes=[("csil",)])
    P.op("act", lambda e: e.activation(out=cact[:], in_=csil[:], func=AF.Copy), reads=[("csil",)], writes=[("cact",)])
    def ada_panel(i, p):
        P.stage = "ada"
        sl, wv = next_panel("ada%d" % p)
        for m in range(8):
            mg = 8 * p + m
            pairs = [(wv[:, k, m * 128:(m + 1) * 128], cact[:, k, :]) for k in range(8)]
            bank = mm_group(2, pairs, reads=[("ws", sl), ("cact",)])
            P.op("dve", lambda e, mg=mg, bank=bank: e.tensor_scalar(
                out=modt[:, i, mg, :], in0=ps[:, bank, 0:2], scalar1=ppc("ada_b", i * 48 + mg), scalar2=None, op0=ALU.add),
                reads=[("ps", bank), ("pp",)], writes=[("mod",)])

    def ada_derived(i):
        for n, (gname, sc0) in enumerate((("nmix", 8), ("nmlp", 32))):
            for col in range(2):
                P.op("dve", lambda e, n=n, col=col, sc0=sc0: e.tensor_scalar(
                    out=drv[:, i, n, :, col], in0=modt[:, i, sc0:sc0 + 8, col], scalar1=1.0, scalar2=None, op0=ALU.add),
                    reads=[("mod",)], writes=[("drv",)])
                P.op("dve", lambda e, n=n, col=col, gname=gname: e.tensor_tensor(
                    out=drv[:, i, n, :, col], in0=drv[:, i, n, :, col], in1=pp[:, PP[gname] + 8 * i:PP[gname] + 8 * i + 8], op=ALU.mult),
                    reads=[("drv",), ("pp",)], writes=[("drv",)])
        for n, (bname, li) in enumerate((("bb", 1), ("cb2", 2))):
            if li != i:
                continue
            for col in range(2):
                P.op("dve", lambda e, n=n, col=col, bname=bname, li=li: e.tensor_tensor(
                    out=gbt[:, n, :, col], in0=modt[:, li, 16:24, col], in1=pp[:, PP[bname]:PP[bname] + 8], op=ALU.mult),
                    reads=[("mod",), ("pp",)], writes=[("gb",)])

    for i in range(2):
        for p in range(6):
            ada_panel(i, p)
        ada_derived(i)

    def mod(i, which, k, col):
        return modt[:, i, which * 8 + k, col:col + 1]

    CONST = [("pp",), ("mod",), ("drv",), ("gb",), ("ones",)]
    ALLK = [("x", k) for k in range(8)] + [("h", k, ni) for k in range(8) for ni in range(3)] + [("rstd", ni) for ni in range(3)] + [("hid", k) for k in range(32)] + [("scr", k) for k in range(4)]
    EPSAP = pp[:, PP["eps"]:PP["eps"] + 1]
    ZEROAP = pp[:, PP["zero"]:PP["zero"] + 1]

    def halo_view(t3, k):
        a = t3[:, k, 0:H]
        return bass.AP(a.tensor, a.offset, [list(a.ap[0]), [W - H, 2], [1, H]])

    def hkf(ni):
        return [("h", k, ni) for k in range(8)]
    HK_ALL = [("h", k, ni) for k in range(8) for ni in range(3)]

    def sq_spec(which):
        if which == "h":
            return (lambda k: ht[:, k, :]), (lambda k, ni: ("h", k, ni))
        return (lambda k: hid[:, 24 + k, :]), (lambda k, ni: ("hid", 24 + k))

    def emit_square(which, k, ni, n0, n1):
        tfn, kfn = sq_spec(which)
        P.op("act", lambda e: e.activation(out=tfn(k)[:, n0:n1], in_=xt[:, k, n0:n1], func=AF.Square),
             reads=[("x", k)], writes=[kfn(k, ni)])

    def norm_tile(which, ni, act_fn, out_keys):
        P.stage = "norm"
        rstd = scr[0]
        tfn, kfn = sq_spec(which)
        n0, n1 = cur["NT"][ni]
        pairs = [(ones[:, :], tfn(k)[:, n0:n1]) for k in range(8)]
        bank = mm_group(n1 - n0, pairs, reads=[("ones",)] + [kfn(k, ni) for k in range(8)])
        P.op("act", lambda e: e.activation(out=rstd[:, n0:n1], in_=ps[:, bank, 0:n1 - n0], func=AF.Sqrt, bias=EPSAP, scale=1.0 / D),
             reads=[("ps", bank), ("pp",)], writes=[("rstd", ni)])
        P.op("dve", lambda e: e.reciprocal(out=rstd[:, n0:n1], in_=rstd[:, n0:n1]), reads=[("rstd", ni)], writes=[("rstd", ni)])
        for k in range(8):
            tb = 1 + (k % 2)
            P.op("dve", lambda e, k=k, tb=tb: e.tensor_tensor(out=scr[tb][:, n0:n1], in0=xt[:, k, n0:n1], in1=rstd[:, n0:n1], op=ALU.mult),
                 reads=[("x", k), ("rstd", ni)], writes=[("scr", tb)])
            P.op("act", act_fn(k, n0, n1, scr[tb]), reads=[("scr", tb)] + CONST, writes=out_keys(k, ni))

    def norm_apply(which, presq, act_fn, out_keys):
        if not presq:
            for ni, (n0, n1) in enumerate(cur["NT"]):
                for k in range(8):
                    emit_square(which, k, ni, n0, n1)
        for ni in range(len(cur["NT"])):
            norm_tile(which, ni, act_fn, out_keys)

    def norm_mod_fns(i, n, col):
        return ((lambda k, n0, n1, tbuf: (lambda e: e.activation(out=ht[:, k, n0:n1], in_=tbuf[:, n0:n1], func=AF.Identity,
                                                                 bias=mod(i, 3 * n, k, col), scale=drv[:, i, n, k, col:col + 1]))),
                (lambda k, ni: [("h", k, ni)]))

    def norm_mod(i, n, col, which="h", presq=False):
        a, o = norm_mod_fns(i, n, col)
        norm_apply(which, presq, a, o)

    def norm_mod_hook(i, n, col, which="h"):
        a, o = norm_mod_fns(i, n, col)
        return lambda ni: norm_tile(which, ni, a, o)

    def resid_evac(i, gidx, col, sqw="h"):
        def ev(m, ni, n0, n1, bank, moff=0):
            mm = m + moff
            P.op("dve", lambda e: e.scalar_tensor_tensor(out=xt[:, mm, n0:n1], in0=ps[:, bank, 0:n1 - n0], scalar=mod(i, gidx, mm, col),
                                                         in1=xt[:, mm, n0:n1], op0=ALU.mult, op1=ALU.add),
                 reads=[("ps", bank), ("x", mm)] + CONST, writes=[("x", mm)])
            emit_square(sqw, mm, ni, n0, n1)
        return ev

    def resid_bias_evac(i, gbn, col, sqw="h"):
        def ev(m, ni, n0, n1, bank):
            tb = 1 + (gst["g"] % 3)
            P.op("act", lambda e: e.activation(out=scr[tb][:, 0:n1 - n0], in_=ps[:, bank, 0:n1 - n0], func=AF.Identity,
                                               bias=gbt[:, gbn, m, col:col + 1], scale=mod(i, 2, m, col)),
                 reads=[("ps", bank)] + CONST, writes=[("scr", tb)])
            P.op("dve", lambda e: e.tensor_tensor(out=xt[:, m, n0:n1], in0=xt[:, m, n0:n1], in1=scr[tb][:, 0:n1 - n0], op=ALU.add),
                 reads=[("scr", tb), ("x", m)], writes=[("x", m)])
            emit_square(sqw, m, ni, n0, n1)
        return ev

    def mlp(i, col, which="h", prenormed=False, next_hook=None):
        if not prenormed:
            norm_mod(i, 1, col, which=which, presq=True)
        hk = hkf
        for p in range(4):
            def ev(m, ni, n0, n1, bank, p=p):
                mg = 8 * p + m
                tb = 1 + (gst["g"] % 3)
                P.op("act", lambda e: e.activation(out=scr[tb][:, 0:n1 - n0], in_=ps[:, bank, 0:n1 - n0], func=AF.Relu),
                     reads=[("ps", bank)], writes=[("scr", tb)])
                P.op("dve", lambda e: e.tensor_tensor(out=hid[:, mg, n0:n1], in0=scr[tb][:, 0:n1 - n0], in1=scr[tb][:, 0:n1 - n0], op=ALU.mult),
                     reads=[("scr", tb)], writes=[("hid", mg)])
            linear("up%d" % p, lambda k, n0, n1: ht[:, k, n0:n1], hk, ev, n_major=(p == 0))
        hidk = [("hid", k) for k in range(32)]
        for p in range(4):
            rev = resid_evac(i, 5, col)
            linear("dn%d" % p, lambda k, n0, n1: hid[:, k, n0:n1], hidk,
                   lambda m, ni, n0, n1, bank, p=p: rev(m, ni, n0, n1, bank, moff=2 * p), kc=32, mper=2, mcols=128,
                   tile_hook=(next_hook if p == 3 else None), hook_delay=1)

    def short_conv(i, col, seg, presq, prenormed=False, next_hook=None):
        j = i // 3
        if not prenormed:
            norm_mod(i, 0, col, presq=presq)
        hk = hkf
        rhs = lambda k, n0, n1: ht[:, k, n0:n1]
        linear("win_c", rhs, hk, lambda m, ni, n0, n1, bank: P.op(
            "act", lambda e: e.activation(out=vbuf[:, m, n0:n1], in_=ps[:, bank, 0:n1 - n0], func=AF.Copy),
            reads=[("ps", bank)], writes=[("hid", 8 + 2 * m), ("hid", 9 + 2 * m)]), n_major=True)

        def ev_v(m, ni, n0, n1, bank):
            P.op("dve", lambda e: e.tensor_tensor(out=vbuf[:, m, n0:n1], in0=ps[:, bank, 0:n1 - n0], in1=vbuf[:, m, n0:n1], op=ALU.mult),
                 reads=[("ps", bank), ("hid", 8 + 2 * m), ("hid", 9 + 2 * m)], writes=[("hid", 8 + 2 * m), ("hid", 9 + 2 * m)])
            if ni == 2:
                mk = pp[:, PP["mask"] + seg * 2 * H:PP["mask"] + (seg + 1) * 2 * H].rearrange("p (a b) -> p a b", a=2)
                P.op("dve", lambda e: e.tensor_tensor(out=halo_view(vbuf, m), in0=halo_view(vbuf, m), in1=mk, op=ALU.mult),
                     reads=[("hid", 8 + 2 * m), ("hid", 9 + 2 * m), ("pp",)], writes=[("hid", 8 + 2 * m), ("hid", 9 + 2 * m)])
        linear("win_v", rhs, hk, ev_v)

        def cw(tap, m):
            return ppc("aconv", (j * 3 + tap) * 8 + m)

        def ev_b(m, ni, n0, n1, bank):
            tb = 1 + (m % 2)
            if ni == 0:
                vk = [("hid", 8 + 2 * m), ("hid", 9 + 2 * m)]
                P.op("act", lambda e: e.activation(out=scr[tb][:, 1:W - 1], in_=vbuf[:, m, 1:W - 1], func=AF.Identity, bias=ZEROAP, scale=cw(1, m)),
                     reads=vk + [("pp",)], writes=[("scr", tb)])
                P.op("dve", lambda e: e.scalar_tensor_tensor(out=scr[tb][:, 1:W - 1], in0=vbuf[:, m, 0:W - 2], scalar=cw(0, m),
                                                             in1=scr[tb][:, 1:W - 1], op0=ALU.mult, op1=ALU.add),
                     reads=vk + [("pp",), ("scr", tb)], writes=[("scr", tb)])
                P.op("dve", lambda e: e.scalar_tensor_tensor(out=scr[tb][:, 1:W - 1], in0=vbuf[:, m, 2:W], scalar=cw(2, m),
                                                             in1=scr[tb][:, 1:W - 1], op0=ALU.mult, op1=ALU.add),
                     reads=vk + [("pp",), ("scr", tb)], writes=[("scr", tb)])
                P.op("dve", lambda e: e.tensor_copy(out=halo_view_1(scr[tb]), in_=halo_view_1v(vbuf, m)),
                     reads=vk + [("scr", tb)], writes=[("scr", tb)])
            P.op("dve", lambda e: e.tensor_tensor(out=hid[:, m, n0:n1], in0=ps[:, bank, 0:n1 - n0], in1=scr[tb][:, n0:n1], op=ALU.mult),
                 reads=[("ps", bank), ("scr", tb)], writes=[("hid", m)])
        linear("win_b", rhs, hk, ev_b)
        linear("wout", lambda k, n0, n1: hid[:, k, n0:n1], [("hid", k) for k in range(8)], resid_evac(i, 2, col), tile_hook=next_hook)

    def halo_view_1(t2):
        a = t2[:, 0:1]
        return bass.AP(a.tensor, a.offset, [list(a.ap[0]), [W - 1, 2]])

    def halo_view_1v(t3, k):
        a = t3[:, k, 0:1]
        return bass.AP(a.tensor, a.offset, [list(a.ap[0]), [W - 1, 2]])

    def conformer(i, col, seg, prenormed=False, next_hook=None):
        if not prenormed:
            norm_mod(i, 0, col, presq=True)
        hk = hkf
        rhs = lambda k, n0, n1: ht[:, k, n0:n1]
        vkeys = lambda m: [("hid", 8 + 2 * m), ("hid", 9 + 2 * m)]
        linear("pw1a", rhs, hk, lambda m, ni, n0, n1, bank: P.op(
            "act", lambda e: e.activation(out=vbuf[:, m, n0:n1], in_=ps[:, bank, 0:n1 - n0], func=AF.Identity, bias=ppc("cb1", m), scale=1.0),
            reads=[("ps", bank), ("pp",)], writes=vkeys(m)), n_major=True)

        def ev_g(m, ni, n0, n1, bank):
            tb = 1 + (gst["g"] % 3)
            P.op("act", lambda e: e.activation(out=scr[tb][:, 0:n1 - n0], in_=ps[:, bank, 0:n1 - n0], func=AF.Sigmoid, bias=ppc("cb1", 8 + m), scale=1.0),
                 reads=[("ps", bank), ("pp",)], writes=[("scr", tb)])
            P.op("dve", lambda e: e.tensor_tensor(out=hid[:, m, n0:n1], in0=vbuf[:, m, n0:n1], in1=scr[tb][:, 0:n1 - n0], op=ALU.mult),
                 reads=vkeys(m) + [("scr", tb)], writes=[("hid", m)])
            if ni == 2:
                mk = pp[:, PP["mask"] + seg * 2 * H:PP["mask"] + (seg + 1) * 2 * H].rearrange("p (a b) -> p a b", a=2)
                P.op("dve", lambda e: e.tensor_tensor(out=halo_view(hid, m), in0=halo_view(hid, m), in1=mk, op=ALU.mult),
                     reads=[("hid", m), ("pp",)], writes=[("hid", m)])
        linear("pw1g", rhs, hk, ev_g)
        CT = [(15, 358), (358, 701), (701, W - 15)]
        for m in range(8):
            d = dg[m % 2]
            P.ops("dve", [lambda e, tap=tap, d=d, m=m: e.tensor_scalar(out=d[:, tap, :], in0=ident[:, :], scalar1=ppc("cdw", tap * 8 + m), scalar2=None, op0=ALU.mult)
                           for tap in range(31)], reads=[("ident",), ("pp",), ("x", 0)], writes=[("dg", m % 2)])
            for (n0, n1) in CT:
                pairs = [(d[:, tap, :], hid[:, m, n0 + tap - 15:n1 + tap - 15]) for tap in range(31)]
                bank = mm_group(n1 - n0, pairs, reads=[("dg", m % 2), ("hid", m)])
                P.op("act", lambda e, m=m, n0=n0, n1=n1, bank=bank: e.activation(out=vbuf[:, m, n0:n1], in_=ps[:, bank, 0:n1 - n0], func=AF.Identity, bias=ppc("cdb", m), scale=1.0),
                     reads=[("ps", bank), ("pp",)], writes=vkeys(m))
        mu, rs = scr[0], scr[1]
        for ni, (n0, n1) in enumerate(NT):
            for k in range(8):
                P.op("act", lambda e, k=k, n0=n0, n1=n1: e.activation(out=ht[:, k, n0:n1], in_=vbuf[:, k, n0:n1], func=AF.Copy),
                     reads=vkeys(k), writes=[("h", k, ni)])
                P.op("act", lambda e, k=k, n0=n0, n1=n1: e.activation(out=hid[:, 24 + k, n0:n1], in_=vbuf[:, k, n0:n1], func=AF.Square),
                     reads=vkeys(k), writes=[("hid", 24 + k)])
            b1 = mm_group(n1 - n0, [(ones[:, :], ht[:, k, n0:n1]) for k in range(8)], reads=[("ones",)] + hkf(ni))
            b2 = mm_group(n1 - n0, [(ones[:, :], hid[:, 24 + k, n0:n1]) for k in range(8)], reads=[("ones",)] + [("hid", 24 + k) for k in range(8)])
            nw = n1 - n0
            P.op("act", lambda e, n0=n0, n1=n1, b1=b1, nw=nw: e.activation(out=mu[:, n0:n1], in_=ps[:, b1, 0:nw], func=AF.Identity, bias=ZEROAP, scale=1.0 / D),
                 reads=[("ps", b1)], writes=[("scr", 0)])
            P.op("dve", lambda e, n0=n0, n1=n1: e.tensor_tensor(out=scr[2][:, n0:n1], in0=mu[:, n0:n1], in1=mu[:, n0:n1], op=ALU.mult),
                 reads=[("scr", 0)], writes=[("scr", 2)])
            P.op("dve", lambda e, n0=n0, n1=n1, b2=b2, nw=nw: e.scalar_tensor_tensor(out=rs[:, n0:n1], in0=ps[:, b2, 0:nw], scalar=1.0 / D, in1=scr[2][:, n0:n1],
                                                                              op0=ALU.mult, op1=ALU.subtract),
                 reads=[("ps", b2), ("scr", 2)], writes=[("scr", 1)])
        P.op("dve", lambda e: e.tensor_scalar(out=rs[:, :], in0=rs[:, :], scalar1=0.0, scalar2=None, op0=ALU.max), reads=[("scr", 1)], writes=[("scr", 1)])
        P.op("act", lambda e: e.activation(out=rs[:, :], in_=rs[:, :], func=AF.Sqrt, bias=EPSAP, scale=1.0), reads=[("scr", 1), ("pp",)], writes=[("scr", 1)])
        P.op("dve", lambda e: e.reciprocal(out=rs[:, :], in_=rs[:, :]), reads=[("scr", 1)], writes=[("scr", 1)])
        P.op("dve", lambda e: e.scalar_tensor_tensor(out=mu[:, :], in0=mu[:, :], scalar=-1.0, in1=rs[:, :], op0=ALU.mult, op1=ALU.mult),
             reads=[("scr", 0), ("scr", 1)], writes=[("scr", 0)])
        for k in range(8):
            tb = 2 + (k % 2)
            P.op("dve", lambda e, k=k, tb=tb: e.tensor_tensor(out=scr[tb][:, :], in0=vbuf[:, k, :], in1=rs[:, :], op=ALU.mult),
                 reads=vkeys(k) + [("scr", 1)], writes=[("scr", tb)])
            P.op("dve", lambda e, k=k, tb=tb: e.tensor_tensor(out=scr[tb][:, :], in0=scr[tb][:, :], in1=mu[:, :], op=ALU.add),
                 reads=[("scr", 0), ("scr", tb)], writes=[("scr", tb)])
            P.op("act", lambda e, k=k, tb=tb: e.activation(out=ht[:, k, :], in_=scr[tb][:, :], func=AF.Silu, bias=ppc("lnb", k), scale=ppc("lng", k)),
                 reads=[("scr", tb), ("pp",)], writes=[("h", k, n3) for n3 in range(3)])
        linear("wout", rhs, hk, resid_bias_evac(i, 1, col, sqw="hid"), tile_hook=next_hook)

    def seg_col(seg):
        return 0 if 2 <= seg < 6 else 1

    def load_x_a(seg):
        P.dma("sp", "d_x", xt[:], xs.ap()[seg].rearrange("(k p) w -> p k w", p=128),
              writes=[("x", k) for k in range(8)])

    load_x_a(0)
    for seg in range(NSEGA):
        col = seg_col(seg)
        cur["NT"] = NT_TRIM if seg >= NSEG else NT
        if deferred_ada(seg):
            di, dp = deferred_ada(seg)
            ada_panel(di, dp)
            if dp == 5:
                ada_derived(di)
        short_conv(0, col, seg, presq=False, next_hook=norm_mod_hook(0, 1, col))
        mlp(0, col, prenormed=True, next_hook=norm_mod_hook(1, 0, col))
        if seg < NSEG:
            P.dma("sp", "d_xst", xd.ap()[seg].rearrange("(k p) w -> p k w", p=128), xt[:],
                  reads=[("x", k) for k in range(8)], writes=[("xd", seg)])
        if seg + 1 < NSEGA:
            load_x_a(seg + 1)
        for tb in range(8):
            c0 = H + 128 * tb
            yb = yzs[tb % 4]
            ykeys = [("hid", 8 * (tb % 4)), ("hid", 1 + 8 * (tb % 4))]
            for g in range(4):
                pairs = [(ht[:, 2 * g + kk, c0:c0 + 128], cs_t[:, kk, :]) for kk in range(2)]
                bank = mm_group(512, pairs, reads=[("cs",)] + [("h", 2 * g + kk, n3) for kk in range(2) for n3 in range(3)])
                eng = "act" if g % 2 == 0 else "dve"
                outv = yb[:, 2 * g:2 * g + 2, :, :].rearrange("p c y q -> p y c q")
                inv = ps[:, bank, :].rearrange("p (y c q) -> p y c q", y=2, c=2)
                if eng == "act":
                    P.op("act", lambda e, outv=outv, inv=inv: e.activation(out=outv, in_=inv, func=AF.Copy), reads=[("ps", bank)], writes=ykeys)
                else:
                    P.op("dve", lambda e, outv=outv, inv=inv: e.tensor_copy(out=outv, in_=inv), reads=[("ps", bank)], writes=ykeys)
            if not (2 <= seg < 6):
                ps_i = seg if seg < 2 else seg - 4
                tt0 = ps_i * T + 128 * tb
                dst = yzsd.ap().rearrange("c t e -> t c e")[tt0:tt0 + 128, :, :]
                P.dma("sp", "d_yz%d" % (tb % 4), dst, yb[:].rearrange("p c y q -> p c (y q)"), reads=ykeys, writes=[("yzsd",)])
            else:
                t0 = (seg - 2) * T + 128 * tb
                P.dma("sp", "d_yz%d" % (tb % 4), yzp.ap()[t0:t0 + 128, :, :], yb[:].rearrange("p c y q -> p c (y q)"), reads=ykeys, writes=[("yzp",)])

    cur["NT"] = NT
    YK = ("phB", 0)
    AK = ("phB", 1)
    P.dma("pool", "d_w1s", w1s_t[:], tab_w1s.ap().rearrange("p (k n) -> p k n", k=2), writes=[("w1s",)] + ALLK)
    P.dma("pool", "d_e2w", e2w[:], tab_e2w.ap(), writes=[("e2w",)] + ALLK)

    def stage1(w1tile, w1key):
        for q2 in range(64):
            bank = gst["g"] % 8
            gst["g"] += 1
            fns = []
            for qq in range(2):
                q = 2 * q2 + qq
                o = ps[:, bank, 256 * qq:256 * qq + 256]
                fns.append(lambda e, o=o, q=q: e.matmul(o, lhsT=Yt[:, :, 0, q], rhs=w1tile[:, 0, :], start=True, stop=False))
                fns.append(lambda e, o=o, q=q: e.matmul(o, lhsT=Yt[:, :, 1, q], rhs=w1tile[:, 1, :], start=False, stop=True))
            P.ops("pe", fns, reads=[YK, w1key], writes=[("ps", bank)])
            outv = At[:, 2 * q2:2 * q2 + 2, :]
            inv = ps[:, bank, :].rearrange("p (q n) -> p q n", q=2)
            if q2 % 2 == 0:
                P.op("act", lambda e, outv=outv, inv=inv: e.activation(out=outv, in_=inv, func=AF.Copy), reads=[("ps", bank)], writes=[AK])
            else:
                P.op("dve", lambda e, outv=outv, inv=inv: e.tensor_copy(out=outv, in_=inv), reads=[("ps", bank)], writes=[AK])

    def stage2_full(S, nb, c4n, fT):
        Sp = S + 2 * H
        P.op("dve", lambda e: e.memset(fT[:, :, 0:H], 0.0), reads=[], writes=[YK])
        P.op("dve", lambda e: e.memset(fT[:, :, H + S:Sp], 0.0), reads=[], writes=[YK])
        for sl4 in range(4):
            sl, ev = next_panel("e2_%d" % sl4)
            for kb in range(8):
                bank = gst["g"] % 8
                gst["g"] += 1
                fns = []
                for jj in range(4):
                    kl = 4 * kb + jj
                    k1 = 32 * sl4 + kl
                    o = ps[:, bank, 128 * jj:128 * jj + 128]
                    fns.append(lambda e, o=o, k1=k1, kl=kl, ev=ev: e.matmul(o, lhsT=At[:, :, k1], rhs=ev[:, kl, 0:128], start=True, stop=False))
                    fns.append(lambda e, o=o, k1=k1, kl=kl, ev=ev: e.matmul(o, lhsT=At[:, :, 128 + k1], rhs=ev[:, kl, 128:256], start=False, stop=True))
                P.ops("pe", fns, reads=[AK, ("ws", sl)], writes=[("ps", bank)])
                k1b = 32 * sl4 + 4 * kb
                a = fT[:, 0, H + k1b:H + k1b + 1]
                outv = bass.AP(a.tensor, a.offset, [list(a.ap[0]), [1, 4], [Sp, c4n], [128, nb]])
                inv = ps[:, bank, :].rearrange("p (j c k) -> p j c k", j=4, c=c4n)
                if kb % 2 == 0:
                    P.op("act", lambda e, outv=outv, inv=inv: e.activation(out=outv, in_=inv, func=AF.Copy), reads=[("ps", bank)], writes=[YK])
                else:
                    P.op("dve", lambda e, outv=outv, inv=inv: e.tensor_copy(out=outv, in_=inv), reads=[("ps", bank)], writes=[YK])

    def stage2_win(i):
        fw = fwc[i % 2]
        fk = ("fwc", i % 2)
        for g16 in range(8):
            bank = gst["g"] % 8
            gst["g"] += 1
            fns = []
            for jj in range(16):
                k1 = 16 * g16 + jj
                o = ps[:, bank, NKK * jj:NKK * jj + NKK]
                fns.append(lambda e, o=o, k1=k1: e.matmul(o, lhsT=At[:, :, k1], rhs=e2w[:, k1, 0:NKK], start=True, stop=False))
                fns.append(lambda e, o=o, k1=k1: e.matmul(o, lhsT=At[:, :, 128 + k1], rhs=e2w[:, k1, NKK:2 * NKK], start=False, stop=True))
            P.ops("pe", fns, reads=[AK, ("e2w",)], writes=[("ps", bank)])
            a = fw[:, 16 * g16:16 * g16 + 1]
            outv = bass.AP(a.tensor, a.offset, [list(a.ap[0]), [1, 16], [128, NKK]])
            inv = ps[:, bank, 0:16 * NKK].rearrange("p (j k) -> p j k", j=16)
            if g16 % 2 == 0:
                P.op("act", lambda e, outv=outv, inv=inv: e.activation(out=outv, in_=inv, func=AF.Copy), reads=[("ps", bank)], writes=[fk])
            else:
                P.op("dve", lambda e, outv=outv, inv=inv: e.tensor_copy(out=outv, in_=inv), reads=[("ps", bank)], writes=[fk])
        P.dma("sp", "d_fsd%d" % (i % 2), fsd.ap()[i], fw[:, :], reads=[fk], writes=[("fsd",)])

    for hh in range(2):
        src = yzp.ap().rearrange("(a b) c e -> a b c e", b=32)
        for c4 in range(4):
            dstv = Yt[:, 32 * c4:32 * c4 + 32, :, :].rearrange("p b y q -> p b (y q)")
            P.dma("sp", "d_ytl", dstv, src[:, :, 4 * hh + c4, :], reads=[("yzp",)], writes=[YK] + (ALLK if (hh == 0 and c4 == 0) else []))
        stage1(w1_t, ("w1",))
        stage2_full(SP, 32, 4, fTp)
        for c4 in range(4):
            P.dma("sp", "d_fp", fpd.ap()[4 * hh + c4], fTp[:, c4, :], reads=[YK], writes=[("fpd",)])
    def load_yt_sample(i):
        srcv = yzsd.ap()[i].rearrange("(a b) e -> a (b e)", b=128)
        P.dma("sp", "d_ytl", Yt[:, :, :, :].rearrange("p b y q -> p (b y q)"), srcv, reads=[("yzsd",)], writes=[YK])

    load_yt_sample(0)
    for i in range(8):
        stage1(w1s_t, ("w1s",))
        if i + 1 < 8:
            load_yt_sample(i + 1)
        stage2_win(i)

    order = [2, 3, 4, 5, 0, 1]
    PHBK = [("phB", 0), ("phB", 1), ("e2w",), ("w1s",), ("fwc", 0), ("fwc", 1)]
    def load_c(seg):
        P.dma("sp", "d_x", xt[:], xd.ap()[seg].rearrange("(k p) w -> p k w", p=128),
              reads=[("xd", seg)], writes=[("x", k) for k in range(8)] + PHBK)
        if seg >= 2:
            c0 = (seg - 2) * T
            fsrc = fpd.ap()[:, :, c0:c0 + W].rearrange("k p w -> p k w")
            fk = ("fpd",)
        else:
            c0 = seg * T
            fsrc = fsd.ap()[:, :, FOFF + c0:FOFF + c0 + W].rearrange("k p w -> p k w")
            fk = ("fsd",)
        P.dma("sp", "d_f", hid[:, 0:8, :], fsrc, reads=[fk], writes=[("hid", k) for k in range(8)] + PHBK)

    load_c(order[0])
    for oi, seg in enumerate(order):
        col = seg_col(seg)
        linear("wout", lambda k, n0, n1: hid[:, k, n0:n1], [("hid", k) for k in range(8)], resid_bias_evac(1, 0, col),
               tile_hook=norm_mod_hook(1, 1, col))
        mlp(1, col, prenormed=True, next_hook=norm_mod_hook(2, 0, col))
        conformer(2, col, seg, prenormed=True, next_hook=norm_mod_hook(2, 1, col, which="hid"))
        mlp(2, col, which="hid", prenormed=True, next_hook=norm_mod_hook(3, 0, col))
        short_conv(3, col, seg, presq=True, prenormed=True, next_hook=norm_mod_hook(3, 1, col))
        mlp(3, col, prenormed=True)
        norm_apply("h", True,
                   lambda k, n0, n1, tbuf: (lambda e: e.activation(out=vbuf[:, k, n0:n1], in_=tbuf[:, n0:n1], func=AF.Identity, bias=ZEROAP, scale=ppc("fin", k))),
                   lambda k, ni: [("hid", 8 + 2 * k), ("hid", 9 + 2 * k)])
        if oi + 1 < len(order):
            load_c(order[oi + 1])
        P.dma("sp", "d_out", yout.ap()[:, seg * T:(seg + 1) * T].rearrange("(k p) t -> p k t", p=128), vbuf[:, :, H:H + T],
              reads=[("hid", 8 + kk) for kk in range(16)], writes=[("yout",)])
    assert pst["i"] == len(panels), (pst["i"], len(panels))
    P.final_wait("sp")

    with ExitStack() as es:
        for name in sorted(P.semnames):
            P.sems[name] = es.enter_context(nc.semaphore(name))
        block = es.enter_context(nc.Block())

        @block.tensor
        def _(e):
            for f in P.q["pe"]:
                f(e)

        @block.scalar
        def _(e):
            for f in P.q["act"]:
                f(e)

        @block.vector
        def _(e):
            for f in P.q["dve"]:
                f(e)

        @block.gpsimd
        def _(e):
            for f in P.q["pool"]:
                f(e)

        @block.sync
        def _(e):
            for f in P.q["sp"]:
                f(e)
    return nc


def _vec(v, n):
    return np.ascontiguousarray(np.asarray(v, np.float32).reshape(n, 128).T)


def _tables():
    a = np.arange(128)
    ang1 = 2 * np.pi * np.outer(a, a) / 128
    C1, S1 = np.cos(ang1), np.sin(ang1)
    w1 = np.concatenate([C1, S1, -S1, C1], 1).astype(np.float32)
    c = np.arange(256)
    angc = 2 * np.pi * np.outer(c, c) / 256
    cs = (np.concatenate([np.cos(angc), np.sin(angc)], 1) / 16.0).astype(np.float32)

    def e2(S):
        nb = S // 128
        c4n = 128 // nb
        b = np.arange(nb)
        k = np.arange(128)[:, None] + 128 * np.arange(nb)[None, :]
        ang = 2 * np.pi * b[:, None, None] * k[None] / S
        sc = 1.0 / np.sqrt(S)
        E = np.zeros((c4n, nb, 128, 2, c4n, nb), np.float32)
        for c4 in range(c4n):
            E[c4, :, :, 0, c4, :] = np.cos(ang) * sc
            E[c4, :, :, 1, c4, :] = -np.sin(ang) * sc
        return E.reshape(128, 128, 2, 128)
    return cs, w1, e2(SP)


_CACHE = {}


def kernel(x_prompt, x_sample, c_prompt, c_sample, ada_w, ada_b, norm_mix, norm_mlp,
           a_w_in, a_conv_w, a_w_out, b_w_out, b_b_out,
           c_w_pw1, c_b_pw1, c_dw_w, c_dw_b, c_ln_g, c_ln_b, c_w_pw2, c_b_pw2,
           mlp_w_up, mlp_w_down, final_norm):
    f32 = lambda a: np.ascontiguousarray(np.asarray(a, np.float32))
    x_prompt, x_sample = f32(x_prompt), f32(x_sample)
    if "nc" not in _CACHE:
        _CACHE["nc"] = build_program()
        _CACHE["tabs"] = _tables()
    nc = _CACHE["nc"]
    cs, w1, e2p = _CACHE["tabs"]
    shared = {
        "ada_w": f32(ada_w), "a_w_in": f32(a_w_in), "a_w_out": f32(a_w_out), "b_w_out": f32(b_w_out)[0],
        "c_w_pw1": f32(c_w_pw1)[0], "c_w_pw2": f32(c_w_pw2)[0], "mlp_w_up": f32(mlp_w_up), "mlp_w_down": f32(mlp_w_down),
        "tab_cs": cs, "tab_w1": w1, "tab_e2p": e2p, "tab_id": np.eye(128, dtype=np.float32),
    }
    ppbase = np.zeros((128, NPP), np.float32)

    def put(name, arr):
        ppbase[:, PP[name]:PP[name] + arr.shape[1]] = arr
    put("ada_b", np.concatenate([_vec(np.asarray(ada_b)[i], 48) for i in range(4)], 1))
    put("nmix", np.concatenate([_vec(np.asarray(norm_mix)[i], 8) for i in range(4)], 1))
    put("nmlp", np.concatenate([_vec(np.asarray(norm_mlp)[i], 8) for i in range(4)], 1))
    put("fin", _vec(final_norm, 8))
    put("aconv", np.concatenate([_vec(np.asarray(a_conv_w)[j, t], 8) for j in range(2) for t in range(3)], 1))
    put("bb", _vec(np.asarray(b_b_out)[0], 8))
    put("cb1", _vec(np.asarray(c_b_pw1)[0], 16))
    put("cdw", np.concatenate([_vec(np.asarray(c_dw_w)[0, t], 8) for t in range(31)], 1))
    put("cdb", _vec(np.asarray(c_dw_b)[0], 8))
    put("lng", _vec(np.asarray(c_ln_g)[0], 8))
    put("lnb", _vec(np.asarray(c_ln_b)[0], 8))
    put("cb2", _vec(np.asarray(c_b_pw2)[0], 8))
    ppbase[:, PP["eps"]] = EPS

    in_maps = []
    for j in range(8):
        own = [2 * j, 2 * j + 1]
        samp_order = own + [g for g in range(16) if g not in own]
        xsj = np.zeros((NSEGA, D, W), np.float32)
        mask = np.zeros((NSEGA, 2 * H), np.float32)
        for seg in range(NSEGA):
            if 2 <= seg < 6:
                src, L, t0 = x_prompt[j], SP, T * (seg - 2) - H
            else:
                g = samp_order[seg if seg < 2 else seg - 4]
                src, L, t0 = x_sample[0], SS, T * g - H
            lo, hi = max(t0, 0), min(t0 + W, L)
            xsj[seg][:, lo - t0:hi - t0] = src[lo:hi].T
            tl = t0 + np.arange(H)
            tr = t0 + W - H + np.arange(H)
            mask[seg, :H] = ((tl >= 0) & (tl < L))
            mask[seg, H:] = ((tr >= 0) & (tr < L))
        ppj = ppbase.copy()
        ppj[:, PP["mask"]:PP["mask"] + NSEGA * 2 * H] = mask.reshape(1, -1)
        cj = np.stack([np.asarray(c_prompt, np.float32)[j], np.asarray(c_sample, np.float32)[0]], 1)
        ppj[:, PP["c"]:PP["c"] + 16] = cj.reshape(8, 128, 2).transpose(1, 0, 2).reshape(128, 16)
        a_loc = np.arange(128)
        a_glob = 8 * np.asarray(samp_order)[a_loc // 8] + (a_loc % 8)
        w1s = np.ascontiguousarray(w1[a_glob, :])
        b = np.arange(128)[:, None, None]
        kk = 128 * (16 * j - 1 + np.arange(NKK))[None, None, :] + np.arange(128)[None, :, None]
        ang = 2 * np.pi * b * kk / SS
        valid = ((kk >= 0) & (kk < SS)).astype(np.float64) / np.sqrt(SS)
        e2w = np.concatenate([np.cos(ang) * valid, -np.sin(ang) * valid], 2).astype(np.float32)
        d = dict(shared)
        d.update({"xs": xsj, "pp": ppj, "tab_w1s": w1s, "tab_e2w": e2w})
        in_maps.append(d)
    res = run_bass_kernel_spmd(nc, in_maps, core_ids=list(range(8)))
    y_prompt = np.zeros((8, SP, D), np.float32)
    y_sample = np.zeros((1, SS, D), np.float32)
    for j in range(8):
        y = res.results[j]["y"]
        y_sample[0, 2048 * j:2048 * j + 2048] = y[:, 0:2048].T
        y_prompt[j] = y[:, 2048:].T
    return (y_prompt, y_sample)
```

```python
import numpy as np
from contextlib import ExitStack
import concourse.bass as bass
import concourse.mybir as mybir
from concourse.bass_utils import run_bass_kernel_spmd

F32 = mybir.dt.float32
BF16 = mybir.dt.bfloat16
I32 = mybir.dt.int32
AF = mybir.ActivationFunctionType
ALU = mybir.AluOpType

D = 1024
KC = 8
W = 1058
H = 17
T = 1024
NSEG = 6
NSEGA = 20
NKK = 18
FOFF = 111
NT = [(0, 358), (358, 700), (700, 1058)]
NT_OWN = [(17, 358), (358, 700), (700, 1041)]
NT_TRIM = [(16, 358), (358, 700), (700, 1042)]
EPS = 1e-6
NS = 3
SLOT = 8192
SP = 4096
SS = 16384
WIN = 2048 + 2 * H

PP = {}
_o = 0
for _n, _c in [("ada_b", 4 * 48), ("nmix", 32), ("nmlp", 32), ("fin", 8), ("aconv", 2 * 3 * 8), ("bb", 8),
               ("cb1", 16), ("cdw", 31 * 8), ("cdb", 8), ("lng", 8), ("lnb", 8), ("cb2", 8),
               ("mask", NSEGA * 2 * H), ("eps", 1), ("zero", 1), ("c", 16)]:
    PP[_n] = _o
    _o += _c
NPP = _o


class Prog:
    ENG = ("pe", "act", "dve", "pool", "sp")
    SELF_SYNC = {"pe": False, "act": True, "dve": True, "pool": True, "sp": False}

    def __init__(self):
        self.q = {e: [] for e in self.ENG}
        self.cnt = {}
        self.lastw = {}
        self.readers = {}
        self.waited = {e: {} for e in self.ENG}
        self.sems = {}
        self.semnames = set("E_" + e for e in self.ENG)

    def _deps(self, eng, reads, writes):
        d = {}

        def add(s, v):
            if s == "E_" + eng and not self.SELF_SYNC[eng]:
                return
            d[s] = max(d.get(s, 0), v)
        for r in reads:
            for s, v in self.lastw.get(r, {}).items():
                add(s, v)
        for w in writes:
            for s, v in self.lastw.get(w, {}).items():
                add(s, v)
            for s, v in self.readers.get(w, {}).items():
                add(s, v)
        return d

    def _waits(self, eng, d):
        for s, v in d.items():
            if self.waited[eng].get(s, 0) >= v:
                continue
            self.waited[eng][s] = v
            self.q[eng].append(lambda e, s=s, v=v: e.wait_ge(self.sems[s], v))

    def _record(self, ev, reads, writes):
        s, v = ev
        for r in reads:
            rr = self.readers.setdefault(r, {})
            rr[s] = max(rr.get(s, 0), v)
        for w in writes:
            ww = self.lastw.setdefault(w, {})
            ww[s] = max(ww.get(s, 0), v)
            self.readers[w] = {}

    def ops(self, eng, fns, reads=(), writes=()):
        self._waits(eng, self._deps(eng, reads, writes))
        s = "E_" + eng
        self.cnt[s] = self.cnt.get(s, 0) + 1
        v = self.cnt[s]
        for fn in fns[:-1]:
            self.q[eng].append(lambda e, fn=fn: fn(e))
        last = fns[-1]
        self.q[eng].append(lambda e, fn=last, s=s: fn(e).then_inc(self.sems[s], 1))
        self._record((s, v), reads, writes)

    def op(self, eng, fn, reads=(), writes=()):
        self.ops(eng, [fn], reads, writes)

    def ev(self, qeng, sem, inc, fn, reads=(), writes=()):
        self.semnames.add(sem)
        self._waits(qeng, self._deps(qeng, reads, writes))
        self.cnt[sem] = self.cnt.get(sem, 0) + inc
        v = self.cnt[sem]
        self.q[qeng].append(lambda e, fn=fn, sem=sem, inc=inc: fn(e).then_inc(self.sems[sem], inc))
        self._record((sem, v), reads, writes)

    def dma(self, qeng, sem, out, in_, reads=(), writes=()):
        self.ev(qeng, sem, 16, lambda e, out=out, in_=in_: e.dma_start(out=out, in_=in_), reads, writes)

    def final_wait(self, eng):
        for s, v in self.cnt.items():
            if s.startswith("E_"):
                continue
            if self.waited[eng].get(s, 0) >= v:
                continue
            self.waited[eng][s] = v
            self.q[eng].append(lambda e, s=s, v=v: e.wait_ge(self.sems[s], v))


def build_program():
    nc = bass.Bass("TRN2", target_bir_lowering=False)
    P = Prog()

    def din(name, shape, dt=F32):
        return nc.dram_tensor(name, list(shape), dt, kind="ExternalInput")

    xs = din("xs", [NSEGA, D, W])
    ppd = din("pp", [128, NPP])
    ada_w = din("ada_w", [4, D, 6 * D])
    a_w_in = din("a_w_in", [2, D, 3 * D])
    a_w_out = din("a_w_out", [2, D, D])
    b_w_out = din("b_w_out", [D, D])
    c_w_pw1 = din("c_w_pw1", [D, 2 * D])
    c_w_pw2 = din("c_w_pw2", [D, D])
    w_up = din("mlp_w_up", [4, D, 4 * D])
    w_down = din("mlp_w_down", [4, 4 * D, D])
    tab_cs = din("tab_cs", [256, 512])
    tab_w1 = din("tab_w1", [128, 512])
    tab_e2 = [None, din("tab_e2p", [128, 128, 2, 128])]
    tab_w1s = din("tab_w1s", [128, 512])
    tab_e2w = din("tab_e2w", [128, 128, 2 * NKK])
    tab_id = din("tab_id", [128, 128])
    yout = nc.dram_tensor("y", [D, NSEG * T], F32, kind="ExternalOutput")

    xd = nc.dram_tensor("xd", [NSEG, D, W], F32)
    yzp = nc.dram_tensor("yzp", [SP, 8, 256], BF16)
    yzsd = nc.dram_tensor("yzsd", [8, SS, 256], BF16)
    fpd = nc.dram_tensor("fpd", [8, 128, SP + 2 * H], BF16)
    fsd = nc.dram_tensor("fsd", [8, 128, NKK * 128], BF16)

    off = [16512]

    def alloc(name, shape, dt, at=None):
        nbytes = int(np.prod(shape[1:])) * (4 if dt in (F32, I32) else 2)
        nbytes = (nbytes + 31) // 32 * 32
        if at is None:
            at = off[0]
            off[0] += nbytes
        assert at + nbytes <= 229344, (name, at, nbytes)
        return nc.alloc_sbuf_tensor_at(name, list(shape), dt, offset=at)

    wsl = [alloc("ws%d" % i, [128, SLOT], BF16) for i in range(NS)]
    pp = alloc("pp", [128, NPP], F32)
    modt = alloc("modt", [128, 4, 48, 2], F32)
    drv = alloc("drv", [128, 4, 2, 8, 2], F32)
    gbt = alloc("gbt", [128, 2, 8, 2], F32)
    cact = alloc("cact", [128, 8, 2], BF16)
    csil = alloc("csil", [128, 8, 2], F32)
    ones = alloc("ones", [128, 128], BF16)
    ident = alloc("ident", [128, 128], BF16)
    cs_t = alloc("cs_t", [128, 2, 512], BF16)
    w1_t = alloc("w1_t", [128, 2, 256], BF16)
    base_ph = off[0]
    xt = alloc("xt", [128, 8, W], F32)
    ht = alloc("ht", [128, 8, W], BF16)
    hid_off = off[0]
    hid = alloc("hid", [128, 32, W], BF16)
    CH = W * 2
    vbuf = alloc("vbuf", [128, 8, W], F32, at=hid_off + 8 * CH)
    yzs = [alloc("yzs%d" % i, [128, 8, 2, 128], BF16, at=hid_off + 8 * i * CH) for i in range(4)]
    scr = [alloc("scr%d" % i, [128, W], F32) for i in range(4)]
    dg0_off = off[0]
    dg = [alloc("dg%d" % i, [128, 31, 128], BF16) for i in range(2)]
    dg1_off = dg0_off + 31 * 128 * 2
    Yt = alloc("Yt", [128, 128, 2, 128], BF16, at=base_ph)
    At = alloc("At", [128, 128, 256], BF16, at=base_ph + 65536)
    fTs = alloc("fTs", [128, 1, SS + 2 * H], BF16, at=base_ph)
    fTp = alloc("fTp", [128, 4, SP + 2 * H], BF16, at=base_ph)
    e2w = alloc("e2w", [128, 128, 2 * NKK], BF16, at=base_ph + 131072)
    fwc = [alloc("fwc%d" % i, [128, NKK * 128], BF16, at=base_ph + 140288 + 4608 * i) for i in range(2)]
    w1s_t = alloc("w1s_t", [128, 2, 256], BF16, at=base_ph + 149504)
    ps = nc.alloc_psum_tensor("ps", [128, 8, 512], F32)

    def ppc(name, i=0, n=1):
        o = PP[name] + i
        return pp[:, o:o + n]

    panels = []

    def k1024(w2d, c0):
        return (w2d[:, c0:c0 + 1024].rearrange("(k p) n -> p k n", p=128), (8, 1024))

    def layer_panels(i):
        kind, j = i % 3, i // 3
        pl = []
        if kind == 0:
            w = a_w_in.ap()[j]
            pl += [("win_c", k1024(w, 1024)), ("win_v", k1024(w, 2048)), ("win_b", k1024(w, 0)),
                   ("wout", k1024(a_w_out.ap()[j], 0))]
        elif kind == 1:
            pl += [("wout", k1024(b_w_out.ap(), 0))]
        else:
            pl += [("pw1a", k1024(c_w_pw1.ap(), 0)), ("pw1g", k1024(c_w_pw1.ap(), 1024)),
                   ("wout", k1024(c_w_pw2.ap(), 0))]
        return pl

    def mlp_panels(i):
        pl = [("up%d" % p, k1024(w_up.ap()[i], 1024 * p)) for p in range(4)]
        pl += [("dn%d" % p, (w_down.ap()[i][:, 256 * p:256 * p + 256].rearrange("(k p) n -> p k n", p=128), (32, 256)))
               for p in range(4)]
        return pl

    def e2_panels(v):
        return [("e2_%d" % s, (tab_e2[v].ap()[:, 32 * s:32 * s + 32, :, :].rearrange("p k c n -> p k (c n)"), (32, 256)))
                for s in range(4)]

    def deferred_ada(seg):
        return (2 + (seg - 1) // 6, (seg - 1) % 6) if 1 <= seg <= 12 else None

    for i in range(2):
        panels += [("ada%d" % p, k1024(ada_w.ap()[i], 1024 * p)) for p in range(6)]
    for seg in range(NSEGA):
        if deferred_ada(seg):
            di, dp = deferred_ada(seg)
            panels += [("ada%d" % dp, k1024(ada_w.ap()[di], 1024 * dp))]
        panels += layer_panels(0) + mlp_panels(0)
    panels += e2_panels(1) + e2_panels(1)
    for seg in range(NSEG):
        panels += layer_panels(1) + mlp_panels(1) + layer_panels(2) + mlp_panels(2) + layer_panels(3) + mlp_panels(3)

    pst = {"i": 0, "loaded": 0}

    def next_panel(tag):
        i = pst["i"]
        pst["i"] += 1
        assert panels[i][0] == tag, (i, panels[i][0], tag)
        while pst["loaded"] < min(len(panels), i + NS):
            j = pst["loaded"]
            src, (a, b) = panels[j][1]
            sl = j % NS
            dst = wsl[sl][:, 0:a * b].rearrange("p (a b) -> p a b", a=a)
            P.dma("pool", "w%d" % sl, dst, src, reads=[], writes=[("ws", sl)])
            pst["loaded"] += 1
        sl = i % NS
        a, b = panels[i][1][1]
        return sl, wsl[sl][:, 0:a * b].rearrange("p (a b) -> p a b", a=a)

    gst = {"g": 0}
    cur = {"NT": NT}

    def mm_group(nw, pairs, reads):
        bank = gst["g"] % 8
        gst["g"] += 1
        n = len(pairs)
        fns = [lambda e, l=l, r=r, i=i: e.matmul(ps[:, bank, 0:nw], lhsT=l, rhs=r, start=(i == 0), stop=(i == n - 1))
               for i, (l, r) in enumerate(pairs)]
        P.ops("pe", fns, reads=reads, writes=[("ps", bank)])
        return bank

    def linear(tag, rhs, rhs_keys, evac, kc=8, mper=8, mcols=128, n_major=False, tile_hook=None, hook_delay=3):
        sl, wv = next_panel(tag)
        P.stage = tag
        rk = rhs_keys if callable(rhs_keys) else (lambda ni: rhs_keys)
        if tile_hook is not None:
            n_major = True
        NTc = cur["NT"]
        order = [(m, ni) for ni in range(len(NTc)) for m in range(mper)] if n_major else [(m, ni) for m in range(mper) for ni in range(len(NTc))]
        fire = {}
        if tile_hook is not None:
            for ni in range(len(NTc)):
                fire.setdefault(min((ni + 1) * mper - 1 + hook_delay, len(order) - 1), []).append(ni)
        for gi, (m, ni) in enumerate(order):
            n0, n1 = NTc[ni]
            pairs = [(wv[:, k, m * mcols:(m + 1) * mcols], rhs(k, n0, n1)) for k in range(kc)]
            bank = mm_group(n1 - n0, pairs, reads=[("ws", sl)] + rk(ni))
            evac(m, ni, n0, n1, bank)
            for nj in fire.get(gi, []):
                tile_hook(nj)
                P.stage = tag

    P.dma("sp", "d_pp", pp[:], ppd.ap(), writes=[("pp",)])
    P.dma("pool", "d_cs", cs_t[:], tab_cs.ap().rearrange("(k p) n -> p k n", p=128), writes=[("cs",)])
    P.dma("pool", "d_w1", w1_t[:], tab_w1.ap().rearrange("p (k n) -> p k n", k=2), writes=[("w1",)])
    P.dma("pool", "d_id", ident[:], tab_id.ap(), writes=[("ident",)])
    P.op("dve", lambda e: e.memset(ones[:], 1.0), writes=[("ones",)])
    cv = pp[:, PP["c"]:PP["c"] + 16].rearrange("p (k c) -> p k c", k=8)
    P.op("act", lambda e: e.activation(out=csil[:], in_=cv, func=AF.Silu), reads=[("pp",)], writes=[("csil",)])
    P.op("act", lambda e: e.activation(out=cact[:], in_=csil[:], func=AF.Copy), reads=[("csil",)], writes=[("cact",)])
    def ada_panel(i, p):
        P.stage = "ada"
        sl, wv = next_panel("ada%d" % p)
        for m in range(8):
            mg = 8 * p + m
            pairs = [(wv[:, k, m * 128:(m + 1) * 128], cact[:, k, :]) for k in range(8)]
            bank = mm_group(2, pairs, reads=[("ws", sl), ("cact",)])
            P.op("dve", lambda e, mg=mg, bank=bank: e.tensor_scalar(
                out=modt[:, i, mg, :], in0=ps[:, bank, 0:2], scalar1=ppc("ada_b", i * 48 + mg), scalar2=None, op0=ALU.add),
                reads=[("ps", bank), ("pp",)], writes=[("mod",)])

    def ada_derived(i):
        for n, (gname, sc0) in enumerate((("nmix", 8), ("nmlp", 32))):
            for col in range(2):
                P.op("dve", lambda e, n=n, col=col, sc0=sc0: e.tensor_scalar(
                    out=drv[:, i, n, :, col], in0=modt[:, i, sc0:sc0 + 8, col], scalar1=1.0, scalar2=None, op0=ALU.add),
                    reads=[("mod",)], writes=[("drv",)])
                P.op("dve", lambda e, n=n, col=col, gname=gname: e.tensor_tensor(
                    out=drv[:, i, n, :, col], in0=drv[:, i, n, :, col], in1=pp[:, PP[gname] + 8 * i:PP[gname] + 8 * i + 8], op=ALU.mult),
                    reads=[("drv",), ("pp",)], writes=[("drv",)])
        for n, (bname, li) in enumerate((("bb", 1), ("cb2", 2))):
            if li != i:
                continue
            for col in range(2):
                P.op("dve", lambda e, n=n, col=col, bname=bname, li=li: e.tensor_tensor(
                    out=gbt[:, n, :, col], in0=modt[:, li, 16:24, col], in1=pp[:, PP[bname]:PP[bname] + 8], op=ALU.mult),
                    reads=[("mod",), ("pp",)], writes=[("gb",)])

    for i in range(2):
        for p in range(6):
            ada_panel(i, p)
        ada_derived(i)

    def mod(i, which, k, col):
        return modt[:, i, which * 8 + k, col:col + 1]

    CONST = [("pp",), ("mod",), ("drv",), ("gb",), ("ones",)]
    ALLK = [("x", k) for k in range(8)] + [("h", k, ni) for k in range(8) for ni in range(3)] + [("rstd", ni) for ni in range(3)] + [("hid", k) for k in range(32)] + [("scr", k) for k in range(4)]
    EPSAP = pp[:, PP["eps"]:PP["eps"] + 1]
    ZEROAP = pp[:, PP["zero"]:PP["zero"] + 1]

    def halo_view(t3, k):
        a = t3[:, k, 0:H]
        return bass.AP(a.tensor, a.offset, [list(a.ap[0]), [W - H, 2], [1, H]])

    def hkf(ni):
        return [("h", k, ni) for k in range(8)]
    HK_ALL = [("h", k, ni) for k in range(8) for ni in range(3)]

    def sq_spec(which):
        if which == "h":
            return (lambda k: ht[:, k, :]), (lambda k, ni: ("h", k, ni))
        return (lambda k: hid[:, 24 + k, :]), (lambda k, ni: ("hid", 24 + k))

    def emit_square(which, k, ni, n0, n1):
        tfn, kfn = sq_spec(which)
        P.op("act", lambda e: e.activation(out=tfn(k)[:, n0:n1], in_=xt[:, k, n0:n1], func=AF.Square),
             reads=[("x", k)], writes=[kfn(k, ni)])

    def norm_tile(which, ni, act_fn, out_keys):
        P.stage = "norm"
        rstd = scr[0]
        tfn, kfn = sq_spec(which)
        n0, n1 = cur["NT"][ni]
        pairs = [(ones[:, :], tfn(k)[:, n0:n1]) for k in range(8)]
        bank = mm_group(n1 - n0, pairs, reads=[("ones",)] + [kfn(k, ni) for k in range(8)])
        P.op("act", lambda e: e.activation(out=rstd[:, n0:n1], in_=ps[:, bank, 0:n1 - n0], func=AF.Sqrt, bias=EPSAP, scale=1.0 / D),
             reads=[("ps", bank), ("pp",)], writes=[("rstd", ni)])
        P.op("dve", lambda e: e.reciprocal(out=rstd[:, n0:n1], in_=rstd[:, n0:n1]), reads=[("rstd", ni)], writes=[("rstd", ni)])
        for k in range(8):
            tb = 1 + (k % 2)
            P.op("dve", lambda e, k=k, tb=tb: e.tensor_tensor(out=scr[tb][:, n0:n1], in0=xt[:, k, n0:n1], in1=rstd[:, n0:n1], op=ALU.mult),
                 reads=[("x", k), ("rstd", ni)], writes=[("scr", tb)])
            P.op("act", act_fn(k, n0, n1, scr[tb]), reads=[("scr", tb)] + CONST, writes=out_keys(k, ni))

    def norm_apply(which, presq, act_fn, out_keys):
        if not presq:
            for ni, (n0, n1) in enumerate(cur["NT"]):
                for k in range(8):
                    emit_square(which, k, ni, n0, n1)
        for ni in range(len(cur["NT"])):
            norm_tile(which, ni, act_fn, out_keys)

    def norm_mod_fns(i, n, col):
        return ((lambda k, n0, n1, tbuf: (lambda e: e.activation(out=ht[:, k, n0:n1], in_=tbuf[:, n0:n1], func=AF.Identity,
                                                                 bias=mod(i, 3 * n, k, col), scale=drv[:, i, n, k, col:col + 1]))),
                (lambda k, ni: [("h", k, ni)]))

    def norm_mod(i, n, col, which="h", presq=False):
        a, o = norm_mod_fns(i, n, col)
        norm_apply(which, presq, a, o)

    def norm_mod_hook(i, n, col, which="h"):
        a, o = norm_mod_fns(i, n, col)
        return lambda ni: norm_tile(which, ni, a, o)

    def resid_evac(i, gidx, col, sqw="h"):
        def ev(m, ni, n0, n1, bank, moff=0):
            mm = m + moff
            P.op("dve", lambda e: e.scalar_tensor_tensor(out=xt[:, mm, n0:n1], in0=ps[:, bank, 0:n1 - n0], scalar=mod(i, gidx, mm, col),
                                                         in1=xt[:, mm, n0:n1], op0=ALU.mult, op1=ALU.add),
                 reads=[("ps", bank), ("x", mm)] + CONST, writes=[("x", mm)])
            emit_square(sqw, mm, ni, n0, n1)
        return ev

    def resid_bias_evac(i, gbn, col, sqw="h"):
        def ev(m, ni, n0, n1, bank):
            tb = 1 + (gst["g"] % 3)
            P.op("act", lambda e: e.activation(out=scr[tb][:, 0:n1 - n0], in_=ps[:, bank, 0:n1 - n0], func=AF.Identity,
                                               bias=gbt[:, gbn, m, col:col + 1], scale=mod(i, 2, m, col)),
                 reads=[("ps", bank)] + CONST, writes=[("scr", tb)])
            P.op("dve", lambda e: e.tensor_tensor(out=xt[:, m, n0:n1], in0=xt[:, m, n0:n1], in1=scr[tb][:, 0:n1 - n0], op=ALU.add),
                 reads=[("scr", tb), ("x", m)], writes=[("x", m)])
            emit_square(sqw, m, ni, n0, n1)
        return ev

    def mlp(i, col, which="h", prenormed=False, next_hook=None):
        if not prenormed:
            norm_mod(i, 1, col, which=which, presq=True)
        hk = hkf
        for p in range(4):
            def ev(m, ni, n0, n1, bank, p=p):
                mg = 8 * p + m
                tb = 1 + (gst["g"] % 3)
                P.op("act", lambda e: e.activation(out=scr[tb][:, 0:n1 - n0], in_=ps[:, bank, 0:n1 - n0], func=AF.Relu),
                     reads=[("ps", bank)], writes=[("scr", tb)])
                P.op("dve", lambda e: e.tensor_tensor(out=hid[:, mg, n0:n1], in0=scr[tb][:, 0:n1 - n0], in1=scr[tb][:, 0:n1 - n0], op=ALU.mult),
                     reads=[("scr", tb)], writes=[("hid", mg)])
            linear("up%d" % p, lambda k, n0, n1: ht[:, k, n0:n1], hk, ev, n_major=(p == 0))
        hidk = [("hid", k) for k in range(32)]
        for p in range(4):
            rev = resid_evac(i, 5, col)
            linear("dn%d" % p, lambda k, n0, n1: hid[:, k, n0:n1], hidk,
                   lambda m, ni, n0, n1, bank, p=p: rev(m, ni, n0, n1, bank, moff=2 * p), kc=32, mper=2, mcols=128,
                   tile_hook=(next_hook if p == 3 else None), hook_delay=1)

    def short_conv(i, col, seg, presq, prenormed=False, next_hook=None, wout_nt=None):
        j = i // 3
        if not prenormed:
            norm_mod(i, 0, col, presq=presq)
        hk = hkf
        rhs = lambda k, n0, n1: ht[:, k, n0:n1]
        linear("win_c", rhs, hk, lambda m, ni, n0, n1, bank: P.op(
            "act", lambda e: e.activation(out=vbuf[:, m, n0:n1], in_=ps[:, bank, 0:n1 - n0], func=AF.Copy),
            reads=[("ps", bank)], writes=[("hid", 8 + 2 * m), ("hid", 9 + 2 * m)]), n_major=True)

        def ev_v(m, ni, n0, n1, bank):
            P.op("dve", lambda e: e.tensor_tensor(out=vbuf[:, m, n0:n1], in0=ps[:, bank, 0:n1 - n0], in1=vbuf[:, m, n0:n1], op=ALU.mult),
                 reads=[("ps", bank), ("hid", 8 + 2 * m), ("hid", 9 + 2 * m)], writes=[("hid", 8 + 2 * m), ("hid", 9 + 2 * m)])
            if ni == 2:
                mk = pp[:, PP["mask"] + seg * 2 * H:PP["mask"] + (seg + 1) * 2 * H].rearrange("p (a b) -> p a b", a=2)
                P.op("dve", lambda e: e.tensor_tensor(out=halo_view(vbuf, m), in0=halo_view(vbuf, m), in1=mk, op=ALU.mult),
                     reads=[("hid", 8 + 2 * m), ("hid", 9 + 2 * m), ("pp",)], writes=[("hid", 8 + 2 * m), ("hid", 9 + 2 * m)])
        linear("win_v", rhs, hk, ev_v)

        def cw(tap, m):
            return ppc("aconv", (j * 3 + tap) * 8 + m)

        def ev_b(m, ni, n0, n1, bank):
            tb = 1 + (m % 2)
            if ni == 0:
                vk = [("hid", 8 + 2 * m), ("hid", 9 + 2 * m)]
                P.op("act", lambda e: e.activation(out=scr[tb][:, 1:W - 1], in_=vbuf[:, m, 1:W - 1], func=AF.Identity, bias=ZEROAP, scale=cw(1, m)),
                     reads=vk + [("pp",)], writes=[("scr", tb)])
                P.op("dve", lambda e: e.scalar_tensor_tensor(out=scr[tb][:, 1:W - 1], in0=vbuf[:, m, 0:W - 2], scalar=cw(0, m),
                                                             in1=scr[tb][:, 1:W - 1], op0=ALU.mult, op1=ALU.add),
                     reads=vk + [("pp",), ("scr", tb)], writes=[("scr", tb)])
                P.op("dve", lambda e: e.scalar_tensor_tensor(out=scr[tb][:, 1:W - 1], in0=vbuf[:, m, 2:W], scalar=cw(2, m),
                                                             in1=scr[tb][:, 1:W - 1], op0=ALU.mult, op1=ALU.add),
                     reads=vk + [("pp",), ("scr", tb)], writes=[("scr", tb)])
                P.op("dve", lambda e: e.tensor_copy(out=halo_view_1(scr[tb]), in_=halo_view_1v(vbuf, m)),
                     reads=vk + [("scr", tb)], writes=[("scr", tb)])
            P.op("dve", lambda e: e.tensor_tensor(out=hid[:, m, n0:n1], in0=ps[:, bank, 0:n1 - n0], in1=scr[tb][:, n0:n1], op=ALU.mult),
                 reads=[("ps", bank), ("scr", tb)], writes=[("hid", m)])
        linear("win_b", rhs, hk, ev_b)
        if wout_nt is not None:
            cur["NT"] = wout_nt
        linear("wout", lambda k, n0, n1: hid[:, k, n0:n1], [("hid", k) for k in range(8)], resid_evac(i, 2, col), tile_hook=next_hook)

    def halo_view_1(t2):
        a = t2[:, 0:1]
        return bass.AP(a.tensor, a.offset, [list(a.ap[0]), [W - 1, 2]])

    def halo_view_1v(t3, k):
        a = t3[:, k, 0:1]
        return bass.AP(a.tensor, a.offset, [list(a.ap[0]), [W - 1, 2]])

    def conformer(i, col, seg, prenormed=False, next_hook=None):
        if not prenormed:
            norm_mod(i, 0, col, presq=True)
        hk = hkf
        rhs = lambda k, n0, n1: ht[:, k, n0:n1]
        vkeys = lambda m: [("hid", 8 + 2 * m), ("hid", 9 + 2 * m)]
        linear("pw1a", rhs, hk, lambda m, ni, n0, n1, bank: P.op(
            "act", lambda e: e.activation(out=vbuf[:, m, n0:n1], in_=ps[:, bank, 0:n1 - n0], func=AF.Identity, bias=ppc("cb1", m), scale=1.0),
            reads=[("ps", bank), ("pp",)], writes=vkeys(m)), n_major=True)

        def ev_g(m, ni, n0, n1, bank):
            tb = 1 + (gst["g"] % 3)
            P.op("act", lambda e: e.activation(out=scr[tb][:, 0:n1 - n0], in_=ps[:, bank, 0:n1 - n0], func=AF.Sigmoid, bias=ppc("cb1", 8 + m), scale=1.0),
                 reads=[("ps", bank), ("pp",)], writes=[("scr", tb)])
            P.op("dve", lambda e: e.tensor_tensor(out=hid[:, m, n0:n1], in0=vbuf[:, m, n0:n1], in1=scr[tb][:, 0:n1 - n0], op=ALU.mult),
                 reads=vkeys(m) + [("scr", tb)], writes=[("hid", m)])
            if ni == 2:
                mk = pp[:, PP["mask"] + seg * 2 * H:PP["mask"] + (seg + 1) * 2 * H].rearrange("p (a b) -> p a b", a=2)
                P.op("dve", lambda e: e.tensor_tensor(out=halo_view(hid, m), in0=halo_view(hid, m), in1=mk, op=ALU.mult),
                     reads=[("hid", m), ("pp",)], writes=[("hid", m)])
        linear("pw1g", rhs, hk, ev_g)
        CT = [(15, 358), (358, 701), (701, W - 15)]
        for m in range(8):
            d = dg[m % 2]
            P.ops("dve", [lambda e, tap=tap, d=d, m=m: e.tensor_scalar(out=d[:, tap, :], in0=ident[:, :], scalar1=ppc("cdw", tap * 8 + m), scalar2=None, op0=ALU.mult)
                           for tap in range(31)], reads=[("ident",), ("pp",), ("x", 0)], writes=[("dg", m % 2)])
            for (n0, n1) in CT:
                pairs = [(d[:, tap, :], hid[:, m, n0 + tap - 15:n1 + tap - 15]) for tap in range(31)]
                bank = mm_group(n1 - n0, pairs, reads=[("dg", m % 2), ("hid", m)])
                P.op("act", lambda e, m=m, n0=n0, n1=n1, bank=bank: e.activation(out=vbuf[:, m, n0:n1], in_=ps[:, bank, 0:n1 - n0], func=AF.Identity, bias=ppc("cdb", m), scale=1.0),
                     reads=[("ps", bank), ("pp",)], writes=vkeys(m))
        cur["NT"] = NT_TRIM
        mu, rs = scr[0], scr[1]
        for ni, (n0, n1) in enumerate(cur["NT"]):
            for k in range(8):
                P.op("act", lambda e, k=k, n0=n0, n1=n1: e.activation(out=ht[:, k, n0:n1], in_=vbuf[:, k, n0:n1], func=AF.Copy),
                     reads=vkeys(k), writes=[("h", k, ni)])
                P.op("act", lambda e, k=k, n0=n0, n1=n1: e.activation(out=hid[:, 24 + k, n0:n1], in_=vbuf[:, k, n0:n1], func=AF.Square),
                     reads=vkeys(k), writes=[("hid", 24 + k)])
            b1 = mm_group(n1 - n0, [(ones[:, :], ht[:, k, n0:n1]) for k in range(8)], reads=[("ones",)] + hkf(ni))
            b2 = mm_group(n1 - n0, [(ones[:, :], hid[:, 24 + k, n0:n1]) for k in range(8)], reads=[("ones",)] + [("hid", 24 + k) for k in range(8)])
            nw = n1 - n0
            P.op("act", lambda e, n0=n0, n1=n1, b1=b1, nw=nw: e.activation(out=mu[:, n0:n1], in_=ps[:, b1, 0:nw], func=AF.Identity, bias=ZEROAP, scale=1.0 / D),
                 reads=[("ps", b1)], writes=[("scr", 0)])
            P.op("dve", lambda e, n0=n0, n1=n1: e.tensor_tensor(out=scr[2][:, n0:n1], in0=mu[:, n0:n1], in1=mu[:, n0:n1], op=ALU.mult),
                 reads=[("scr", 0)], writes=[("scr", 2)])
            P.op("dve", lambda e, n0=n0, n1=n1, b2=b2, nw=nw: e.scalar_tensor_tensor(out=rs[:, n0:n1], in0=ps[:, b2, 0:nw], scalar=1.0 / D, in1=scr[2][:, n0:n1],
                                                                              op0=ALU.mult, op1=ALU.subtract),
                 reads=[("ps", b2), ("scr", 2)], writes=[("scr", 1)])
        P.op("dve", lambda e: e.tensor_scalar(out=rs[:, :], in0=rs[:, :], scalar1=0.0, scalar2=None, op0=ALU.max), reads=[("scr", 1)], writes=[("scr", 1)])
        P.op("act", lambda e: e.activation(out=rs[:, :], in_=rs[:, :], func=AF.Sqrt, bias=EPSAP, scale=1.0), reads=[("scr", 1), ("pp",)], writes=[("scr", 1)])
        P.op("dve", lambda e: e.reciprocal(out=rs[:, :], in_=rs[:, :]), reads=[("scr", 1)], writes=[("scr", 1)])
        P.op("dve", lambda e: e.scalar_tensor_tensor(out=mu[:, :], in0=mu[:, :], scalar=-1.0, in1=rs[:, :], op0=ALU.mult, op1=ALU.mult),
             reads=[("scr", 0), ("scr", 1)], writes=[("scr", 0)])
        for k in range(8):
            tb = 2 + (k % 2)
            P.op("dve", lambda e, k=k, tb=tb: e.tensor_tensor(out=scr[tb][:, :], in0=vbuf[:, k, :], in1=rs[:, :], op=ALU.mult),
                 reads=vkeys(k) + [("scr", 1)], writes=[("scr", tb)])
            P.op("dve", lambda e, k=k, tb=tb: e.tensor_tensor(out=scr[tb][:, :], in0=scr[tb][:, :], in1=mu[:, :], op=ALU.add),
                 reads=[("scr", 0), ("scr", tb)], writes=[("scr", tb)])
            P.op("act", lambda e, k=k, tb=tb: e.activation(out=ht[:, k, :], in_=scr[tb][:, :], func=AF.Silu, bias=ppc("lnb", k), scale=ppc("lng", k)),
                 reads=[("scr", tb), ("pp",)], writes=[("h", k, n3) for n3 in range(3)])
        linear("wout", rhs, hk, resid_bias_evac(i, 1, col, sqw="hid"), tile_hook=next_hook)

    def seg_col(seg):
        return 0 if 2 <= seg < 6 else 1

    def load_x_a(seg):
        P.dma("sp", "d_x", xt[:], xs.ap()[seg].rearrange("(k p) w -> p k w", p=128),
              writes=[("x", k) for k in range(8)])

    load_x_a(0)
    for seg in range(NSEGA):
        col = seg_col(seg)
        cur["NT"] = NT_TRIM if seg >= NSEG else NT
        if deferred_ada(seg):
            di, dp = deferred_ada(seg)
            ada_panel(di, dp)
            if dp == 5:
                ada_derived(di)
        short_conv(0, col, seg, presq=False, next_hook=norm_mod_hook(0, 1, col))
        mlp(0, col, prenormed=True, next_hook=norm_mod_hook(1, 0, col))
        if seg < NSEG:
            P.dma("sp", "d_xst", xd.ap()[seg].rearrange("(k p) w -> p k w", p=128), xt[:],
                  reads=[("x", k) for k in range(8)], writes=[("xd", seg)])
        if seg + 1 < NSEGA:
            load_x_a(seg + 1)
        for tb in range(8):
            c0 = H + 128 * tb
            yb = yzs[tb % 4]
            ykeys = [("hid", 8 * (tb % 4)), ("hid", 1 + 8 * (tb % 4))]
            for g in range(4):
                pairs = [(ht[:, 2 * g + kk, c0:c0 + 128], cs_t[:, kk, :]) for kk in range(2)]
                bank = mm_group(512, pairs, reads=[("cs",)] + [("h", 2 * g + kk, n3) for kk in range(2) for n3 in range(3)])
                eng = "act" if g % 2 == 0 else "dve"
                outv = yb[:, 2 * g:2 * g + 2, :, :].rearrange("p c y q -> p y c q")
                inv = ps[:, bank, :].rearrange("p (y c q) -> p y c q", y=2, c=2)
                if eng == "act":
                    P.op("act", lambda e, outv=outv, inv=inv: e.activation(out=outv, in_=inv, func=AF.Copy), reads=[("ps", bank)], writes=ykeys)
                else:
                    P.op("dve", lambda e, outv=outv, inv=inv: e.tensor_copy(out=outv, in_=inv), reads=[("ps", bank)], writes=ykeys)
            if not (2 <= seg < 6):
                ps_i = seg if seg < 2 else seg - 4
                tt0 = ps_i * T + 128 * tb
                dst = yzsd.ap().rearrange("c t e -> t c e")[tt0:tt0 + 128, :, :]
                P.dma("sp", "d_yz%d" % (tb % 4), dst, yb[:].rearrange("p c y q -> p c (y q)"), reads=ykeys, writes=[("yzsd",)])
            else:
                t0 = (seg - 2) * T + 128 * tb
                P.dma("sp", "d_yz%d" % (tb % 4), yzp.ap()[t0:t0 + 128, :, :], yb[:].rearrange("p c y q -> p c (y q)"), reads=ykeys, writes=[("yzp",)])

    cur["NT"] = NT
    YK = ("phB", 0)
    AK = ("phB", 1)
    P.dma("pool", "d_w1s", w1s_t[:], tab_w1s.ap().rearrange("p (k n) -> p k n", k=2), writes=[("w1s",)] + ALLK)
    P.dma("pool", "d_e2w", e2w[:], tab_e2w.ap(), writes=[("e2w",)] + ALLK)

    def stage1(w1tile, w1key):
        for q2 in range(64):
            bank = gst["g"] % 8
            gst["g"] += 1
            fns = []
            for qq in range(2):
                q = 2 * q2 + qq
                o = ps[:, bank, 256 * qq:256 * qq + 256]
                fns.append(lambda e, o=o, q=q: e.matmul(o, lhsT=Yt[:, :, 0, q], rhs=w1tile[:, 0, :], start=True, stop=False))
                fns.append(lambda e, o=o, q=q: e.matmul(o, lhsT=Yt[:, :, 1, q], rhs=w1tile[:, 1, :], start=False, stop=True))
            P.ops("pe", fns, reads=[YK, w1key], writes=[("ps", bank)])
            outv = At[:, 2 * q2:2 * q2 + 2, :]
            inv = ps[:, bank, :].rearrange("p (q n) -> p q n", q=2)
            if q2 % 2 == 0:
                P.op("act", lambda e, outv=outv, inv=inv: e.activation(out=outv, in_=inv, func=AF.Copy), reads=[("ps", bank)], writes=[AK])
            else:
                P.op("dve", lambda e, outv=outv, inv=inv: e.tensor_copy(out=outv, in_=inv), reads=[("ps", bank)], writes=[AK])

    def stage2_full(S, nb, c4n, fT):
        Sp = S + 2 * H
        P.op("dve", lambda e: e.memset(fT[:, :, 0:H], 0.0), reads=[], writes=[YK])
        P.op("dve", lambda e: e.memset(fT[:, :, H + S:Sp], 0.0), reads=[], writes=[YK])
        for sl4 in range(4):
            sl, ev = next_panel("e2_%d" % sl4)
            for kb in range(8):
                bank = gst["g"] % 8
                gst["g"] += 1
                fns = []
                for jj in range(4):
                    kl = 4 * kb + jj
                    k1 = 32 * sl4 + kl
                    o = ps[:, bank, 128 * jj:128 * jj + 128]
                    fns.append(lambda e, o=o, k1=k1, kl=kl, ev=ev: e.matmul(o, lhsT=At[:, :, k1], rhs=ev[:, kl, 0:128], start=True, stop=False))
                    fns.append(lambda e, o=o, k1=k1, kl=kl, ev=ev: e.matmul(o, lhsT=At[:, :, 128 + k1], rhs=ev[:, kl, 128:256], start=False, stop=True))
                P.ops("pe", fns, reads=[AK, ("ws", sl)], writes=[("ps", bank)])
                k1b = 32 * sl4 + 4 * kb
                a = fT[:, 0, H + k1b:H + k1b + 1]
                outv = bass.AP(a.tensor, a.offset, [list(a.ap[0]), [1, 4], [Sp, c4n], [128, nb]])
                inv = ps[:, bank, :].rearrange("p (j c k) -> p j c k", j=4, c=c4n)
                if kb % 2 == 0:
                    P.op("act", lambda e, outv=outv, inv=inv: e.activation(out=outv, in_=inv, func=AF.Copy), reads=[("ps", bank)], writes=[YK])
                else:
                    P.op("dve", lambda e, outv=outv, inv=inv: e.tensor_copy(out=outv, in_=inv), reads=[("ps", bank)], writes=[YK])

    def stage2_win(i):
        fw = fwc[i % 2]
        fk = ("fwc", i % 2)
        for g16 in range(8):
            bank = gst["g"] % 8
            gst["g"] += 1
            fns = []
            for jj in range(16):
                k1 = 16 * g16 + jj
                o = ps[:, bank, NKK * jj:NKK * jj + NKK]
                fns.append(lambda e, o=o, k1=k1: e.matmul(o, lhsT=At[:, :, k1], rhs=e2w[:, k1, 0:NKK], start=True, stop=False))
                fns.append(lambda e, o=o, k1=k1: e.matmul(o, lhsT=At[:, :, 128 + k1], rhs=e2w[:, k1, NKK:2 * NKK], start=False, stop=True))
            P.ops("pe", fns, reads=[AK, ("e2w",)], writes=[("ps", bank)])
            a = fw[:, 16 * g16:16 * g16 + 1]
            outv = bass.AP(a.tensor, a.offset, [list(a.ap[0]), [1, 16], [128, NKK]])
            inv = ps[:, bank, 0:16 * NKK].rearrange("p (j k) -> p j k", j=16)
            if g16 % 2 == 0:
                P.op("act", lambda e, outv=outv, inv=inv: e.activation(out=outv, in_=inv, func=AF.Copy), reads=[("ps", bank)], writes=[fk])
            else:
                P.op("dve", lambda e, outv=outv, inv=inv: e.tensor_copy(out=outv, in_=inv), reads=[("ps", bank)], writes=[fk])
        P.dma("sp", "d_fsd%d" % (i % 2), fsd.ap()[i], fw[:, :], reads=[fk], writes=[("fsd",)])

    for hh in range(2):
        src = yzp.ap().rearrange("(a b) c e -> a b c e", b=32)
        for c4 in range(4):
            dstv = Yt[:, 32 * c4:32 * c4 + 32, :, :].rearrange("p b y q -> p b (y q)")
            P.dma("sp", "d_ytl", dstv, src[:, :, 4 * hh + c4, :], reads=[("yzp",)], writes=[YK] + (ALLK if (hh == 0 and c4 == 0) else []))
        stage1(w1_t, ("w1",))
        stage2_full(SP, 32, 4, fTp)
        for c4 in range(4):
            P.dma("sp", "d_fp", fpd.ap()[4 * hh + c4], fTp[:, c4, :], reads=[YK], writes=[("fpd",)])
    def load_yt_sample(i):
        srcv = yzsd.ap()[i].rearrange("(a b) e -> a (b e)", b=128)
        P.dma("sp", "d_ytl", Yt[:, :, :, :].rearrange("p b y q -> p (b y q)"), srcv, reads=[("yzsd",)], writes=[YK])

    load_yt_sample(0)
    for i in range(8):
        stage1(w1s_t, ("w1s",))
        if i + 1 < 8:
            load_yt_sample(i + 1)
        stage2_win(i)

    order = [2, 3, 4, 5, 0, 1]
    PHBK = [("phB", 0), ("phB", 1), ("e2w",), ("w1s",), ("fwc", 0), ("fwc", 1)]
    def load_c(seg):
        P.dma("sp", "d_x", xt[:], xd.ap()[seg].rearrange("(k p) w -> p k w", p=128),
              reads=[("xd", seg)], writes=[("x", k) for k in range(8)] + PHBK)
        if seg >= 2:
            c0 = (seg - 2) * T
            fsrc = fpd.ap()[:, :, c0:c0 + W].rearrange("k p w -> p k w")
            fk = ("fpd",)
        else:
            c0 = seg * T
            fsrc = fsd.ap()[:, :, FOFF + c0:FOFF + c0 + W].rearrange("k p w -> p k w")
            fk = ("fsd",)
        P.dma("sp", "d_f", hid[:, 0:8, :], fsrc, reads=[fk], writes=[("hid", k) for k in range(8)] + PHBK)

    load_c(order[0])
    for oi, seg in enumerate(order):
        col = seg_col(seg)
        cur["NT"] = NT
        linear("wout", lambda k, n0, n1: hid[:, k, n0:n1], [("hid", k) for k in range(8)], resid_bias_evac(1, 0, col),
               tile_hook=norm_mod_hook(1, 1, col))
        mlp(1, col, prenormed=True, next_hook=norm_mod_hook(2, 0, col))
        conformer(2, col, seg, prenormed=True, next_hook=norm_mod_hook(2, 1, col, which="hid"))
        mlp(2, col, which="hid", prenormed=True, next_hook=norm_mod_hook(3, 0, col))
        short_conv(3, col, seg, presq=True, prenormed=True, next_hook=norm_mod_hook(3, 1, col), wout_nt=NT_OWN)
        mlp(3, col, prenormed=True)
        norm_apply("h", True,
                   lambda k, n0, n1, tbuf: (lambda e: e.activation(out=vbuf[:, k, n0:n1], in_=tbuf[:, n0:n1], func=AF.Identity, bias=ZEROAP, scale=ppc("fin", k))),
                   lambda k, ni: [("hid", 8 + 2 * k), ("hid", 9 + 2 * k)])
        if oi + 1 < len(order):
            load_c(order[oi + 1])
        P.dma("sp", "d_out", yout.ap()[:, seg * T:(seg + 1) * T].rearrange("(k p) t -> p k t", p=128), vbuf[:, :, H:H + T],
              reads=[("hid", 8 + kk) for kk in range(16)], writes=[("yout",)])
    assert pst["i"] == len(panels), (pst["i"], len(panels))
    P.final_wait("sp")

    with ExitStack() as es:
        for name in sorted(P.semnames):
            P.sems[name] = es.enter_context(nc.semaphore(name))
        block = es.enter_context(nc.Block())

        @block.tensor
        def _(e):
            for f in P.q["pe"]:
                f(e)

        @block.scalar
        def _(e):
            for f in P.q["act"]:
                f(e)

        @block.vector
        def _(e):
            for f in P.q["dve"]:
                f(e)

        @block.gpsimd
        def _(e):
            for f in P.q["pool"]:
                f(e)

        @block.sync
        def _(e):
            for f in P.q["sp"]:
                f(e)
    return nc


def _vec(v, n):
    return np.ascontiguousarray(np.asarray(v, np.float32).reshape(n, 128).T)


def _tables():
    a = np.arange(128)
    ang1 = 2 * np.pi * np.outer(a, a) / 128
    C1, S1 = np.cos(ang1), np.sin(ang1)
    w1 = np.concatenate([C1, S1, -S1, C1], 1).astype(np.float32)
    c = np.arange(256)
    angc = 2 * np.pi * np.outer(c, c) / 256
    cs = (np.concatenate([np.cos(angc), np.sin(angc)], 1) / 16.0).astype(np.float32)

    def e2(S):
        nb = S // 128
        c4n = 128 // nb
        b = np.arange(nb)
        k = np.arange(128)[:, None] + 128 * np.arange(nb)[None, :]
        ang = 2 * np.pi * b[:, None, None] * k[None] / S
        sc = 1.0 / np.sqrt(S)
        E = np.zeros((c4n, nb, 128, 2, c4n, nb), np.float32)
        for c4 in range(c4n):
            E[c4, :, :, 0, c4, :] = np.cos(ang) * sc
            E[c4, :, :, 1, c4, :] = -np.sin(ang) * sc
        return E.reshape(128, 128, 2, 128)
    return cs, w1, e2(SP)


_CACHE = {}


def kernel(x_prompt, x_sample, c_prompt, c_sample, ada_w, ada_b, norm_mix, norm_mlp,
           a_w_in, a_conv_w, a_w_out, b_w_out, b_b_out,
           c_w_pw1, c_b_pw1, c_dw_w, c_dw_b, c_ln_g, c_ln_b, c_w_pw2, c_b_pw2,
           mlp_w_up, mlp_w_down, final_norm):
    f32 = lambda a: np.ascontiguousarray(np.asarray(a, np.float32))
    x_prompt, x_sample = f32(x_prompt), f32(x_sample)
    if "nc" not in _CACHE:
        _CACHE["nc"] = build_program()
        _CACHE["tabs"] = _tables()
    nc = _CACHE["nc"]
    cs, w1, e2p = _CACHE["tabs"]
    shared = {
        "ada_w": f32(ada_w), "a_w_in": f32(a_w_in), "a_w_out": f32(a_w_out), "b_w_out": f32(b_w_out)[0],
        "c_w_pw1": f32(c_w_pw1)[0], "c_w_pw2": f32(c_w_pw2)[0], "mlp_w_up": f32(mlp_w_up), "mlp_w_down": f32(mlp_w_down),
        "tab_cs": cs, "tab_w1": w1, "tab_e2p": e2p, "tab_id": np.eye(128, dtype=np.float32),
    }
    ppbase = np.zeros((128, NPP), np.float32)

    def put(name, arr):
        ppbase[:, PP[name]:PP[name] + arr.shape[1]] = arr
    put("ada_b", np.concatenate([_vec(np.asarray(ada_b)[i], 48) for i in range(4)], 1))
    put("nmix", np.concatenate([_vec(np.asarray(norm_mix)[i], 8) for i in range(4)], 1))
    put("nmlp", np.concatenate([_vec(np.asarray(norm_mlp)[i], 8) for i in range(4)], 1))
    put("fin", _vec(final_norm, 8))
    put("aconv", np.concatenate([_vec(np.asarray(a_conv_w)[j, t], 8) for j in range(2) for t in range(3)], 1))
    put("bb", _vec(np.asarray(b_b_out)[0], 8))
    put("cb1", _vec(np.asarray(c_b_pw1)[0], 16))
    put("cdw", np.concatenate([_vec(np.asarray(c_dw_w)[0, t], 8) for t in range(31)], 1))
    put("cdb", _vec(np.asarray(c_dw_b)[0], 8))
    put("lng", _vec(np.asarray(c_ln_g)[0], 8))
    put("lnb", _vec(np.asarray(c_ln_b)[0], 8))
    put("cb2", _vec(np.asarray(c_b_pw2)[0], 8))
    ppbase[:, PP["eps"]] = EPS

    in_maps = []
    for j in range(8):
        own = [2 * j, 2 * j + 1]
        samp_order = own + [g for g in range(16) if g not in own]
        xsj = np.zeros((NSEGA, D, W), np.float32)
        mask = np.zeros((NSEGA, 2 * H), np.float32)
        for seg in range(NSEGA):
            if 2 <= seg < 6:
                src, L, t0 = x_prompt[j], SP, T * (seg - 2) - H
            else:
                g = samp_order[seg if seg < 2 else seg - 4]
                src, L, t0 = x_sample[0], SS, T * g - H
            lo, hi = max(t0, 0), min(t0 + W, L)
            xsj[seg][:, lo - t0:hi - t0] = src[lo:hi].T
            tl = t0 + np.arange(H)
            tr = t0 + W - H + np.arange(H)
            mask[seg, :H] = ((tl >= 0) & (tl < L))
            mask[seg, H:] = ((tr >= 0) & (tr < L))
        ppj = ppbase.copy()
        ppj[:, PP["mask"]:PP["mask"] + NSEGA * 2 * H] = mask.reshape(1, -1)
        cj = np.stack([np.asarray(c_prompt, np.float32)[j], np.asarray(c_sample, np.float32)[0]], 1)
        ppj[:, PP["c"]:PP["c"] + 16] = cj.reshape(8, 128, 2).transpose(1, 0, 2).reshape(128, 16)
        a_loc = np.arange(128)
        a_glob = 8 * np.asarray(samp_order)[a_loc // 8] + (a_loc % 8)
        w1s = np.ascontiguousarray(w1[a_glob, :])
        b = np.arange(128)[:, None, None]
        kk = 128 * (16 * j - 1 + np.arange(NKK))[None, None, :] + np.arange(128)[None, :, None]
        ang = 2 * np.pi * b * kk / SS
        valid = ((kk >= 0) & (kk < SS)).astype(np.float64) / np.sqrt(SS)
        e2w = np.concatenate([np.cos(ang) * valid, -np.sin(ang) * valid], 2).astype(np.float32)
        d = dict(shared)
        d.update({"xs": xsj, "pp": ppj, "tab_w1s": w1s, "tab_e2w": e2w})
        in_maps.append(d)
    res = run_bass_kernel_spmd(nc, in_maps, core_ids=list(range(8)))
    y_prompt = np.zeros((8, SP, D), np.float32)
    y_sample = np.zeros((1, SS, D), np.float32)
    for j in range(8):
        y = res.results[j]["y"]
        y_sample[0, 2048 * j:2048 * j + 2048] = y[:, 0:2048].T
        y_prompt[j] = y[:, 2048:].T
    return (y_prompt, y_sample)
```

```python
import numpy as np
from contextlib import ExitStack
import concourse.bass as bass
import concourse.mybir as mybir
from concourse.bass_utils import run_bass_kernel_spmd

F32 = mybir.dt.float32
BF16 = mybir.dt.bfloat16
I32 = mybir.dt.int32
AF = mybir.ActivationFunctionType
ALU = mybir.AluOpType

D = 1024
KC = 8
W = 1058
H = 17
T = 1024
NSEG = 6
NSEGA = 20
NKK = 18
FOFF = 111
NT = [(0, 358), (358, 700), (700, 1058)]
NT_OWN = [(17, 358), (358, 700), (700, 1041)]
NT_TRIM = [(16, 358), (358, 700), (700, 1042)]
EPS = 1e-6
NS = 3
SLOT = 8192
SP = 4096
SS = 16384
WIN = 2048 + 2 * H

PP = {}
_o = 0
for _n, _c in [("ada_b", 4 * 48), ("nmix", 32), ("nmlp", 32), ("fin", 8), ("aconv", 2 * 3 * 8), ("bb", 8),
               ("cb1", 16), ("cdw", 31 * 8), ("cdb", 8), ("lng", 8), ("lnb", 8), ("cb2", 8),
               ("mask", NSEGA * 2 * H), ("eps", 1), ("zero", 1), ("c", 16)]:
    PP[_n] = _o
    _o += _c
NPP = _o


class Prog:
    ENG = ("pe", "act", "dve", "pool", "sp")
    SELF_SYNC = {"pe": False, "act": True, "dve": True, "pool": True, "sp": False}

    def __init__(self):
        self.q = {e: [] for e in self.ENG}
        self.cnt = {}
        self.lastw = {}
        self.readers = {}
        self.waited = {e: {} for e in self.ENG}
        self.sems = {}
        self.semnames = set("E_" + e for e in self.ENG)

    def _deps(self, eng, reads, writes):
        d = {}

        def add(s, v):
            if s == "E_" + eng and not self.SELF_SYNC[eng]:
                return
            d[s] = max(d.get(s, 0), v)
        for r in reads:
            for s, v in self.lastw.get(r, {}).items():
                add(s, v)
        for w in writes:
            for s, v in self.lastw.get(w, {}).items():
                add(s, v)
            for s, v in self.readers.get(w, {}).items():
                add(s, v)
        return d

    def _waits(self, eng, d):
        for s, v in d.items():
            if self.waited[eng].get(s, 0) >= v:
                continue
            self.waited[eng][s] = v
            self.q[eng].append(lambda e, s=s, v=v: e.wait_ge(self.sems[s], v))

    def _record(self, ev, reads, writes):
        s, v = ev
        for r in reads:
            rr = self.readers.setdefault(r, {})
            rr[s] = max(rr.get(s, 0), v)
        for w in writes:
            ww = self.lastw.setdefault(w, {})
            ww[s] = max(ww.get(s, 0), v)
            self.readers[w] = {}

    def ops(self, eng, fns, reads=(), writes=()):
        self._waits(eng, self._deps(eng, reads, writes))
        s = "E_" + eng
        self.cnt[s] = self.cnt.get(s, 0) + 1
        v = self.cnt[s]
        for fn in fns[:-1]:
            self.q[eng].append(lambda e, fn=fn: fn(e))
        last = fns[-1]
        self.q[eng].append(lambda e, fn=last, s=s: fn(e).then_inc(self.sems[s], 1))
        self._record((s, v), reads, writes)

    def op(self, eng, fn, reads=(), writes=()):
        self.ops(eng, [fn], reads, writes)

    def ev(self, qeng, sem, inc, fn, reads=(), writes=()):
        self.semnames.add(sem)
        self._waits(qeng, self._deps(qeng, reads, writes))
        self.cnt[sem] = self.cnt.get(sem, 0) + inc
        v = self.cnt[sem]
        self.q[qeng].append(lambda e, fn=fn, sem=sem, inc=inc: fn(e).then_inc(self.sems[sem], inc))
        self._record((sem, v), reads, writes)

    def dma(self, qeng, sem, out, in_, reads=(), writes=()):
        self.ev(qeng, sem, 16, lambda e, out=out, in_=in_: e.dma_start(out=out, in_=in_), reads, writes)

    def final_wait(self, eng):
        for s, v in self.cnt.items():
            if s.startswith("E_"):
                continue
            if self.waited[eng].get(s, 0) >= v:
                continue
            self.waited[eng][s] = v
            self.q[eng].append(lambda e, s=s, v=v: e.wait_ge(self.sems[s], v))


def build_program():
    nc = bass.Bass("TRN2", target_bir_lowering=False)
    P = Prog()

    def din(name, shape, dt=F32):
        return nc.dram_tensor(name, list(shape), dt, kind="ExternalInput")

    xs = din("xs", [NSEGA, D, W])
    ppd = din("pp", [128, NPP])
    ada_w = din("ada_w", [4, D, 6 * D])
    a_w_in = din("a_w_in", [2, D, 3 * D])
    a_w_out = din("a_w_out", [2, D, D])
    b_w_out = din("b_w_out", [D, D])
    c_w_pw1 = din("c_w_pw1", [D, 2 * D])
    c_w_pw2 = din("c_w_pw2", [D, D])
    w_up = din("mlp_w_up", [4, D, 4 * D])
    w_down = din("mlp_w_down", [4, 4 * D, D])
    tab_cs = din("tab_cs", [256, 512])
    tab_w1 = din("tab_w1", [128, 512])
    tab_e2 = [None, din("tab_e2p", [128, 128, 2, 128])]
    tab_w1s = din("tab_w1s", [128, 512])
    tab_e2w = din("tab_e2w", [128, 128, 2 * NKK])
    tab_id = din("tab_id", [128, 128])
    yout = nc.dram_tensor("y", [D, NSEG * T], F32, kind="ExternalOutput")

    xd = nc.dram_tensor("xd", [NSEG, D, W], F32)
    yzp = nc.dram_tensor("yzp", [SP, 8, 256], BF16)
    yzsd = nc.dram_tensor("yzsd", [8, SS, 256], BF16)
    fpd = nc.dram_tensor("fpd", [8, 128, SP + 2 * H], BF16)
    fsd = nc.dram_tensor("fsd", [8, 128, NKK * 128], BF16)

    off = [16512]

    def alloc(name, shape, dt, at=None):
        nbytes = int(np.prod(shape[1:])) * (4 if dt in (F32, I32) else 2)
        nbytes = (nbytes + 31) // 32 * 32
        if at is None:
            at = off[0]
            off[0] += nbytes
        assert at + nbytes <= 229344, (name, at, nbytes)
        return nc.alloc_sbuf_tensor_at(name, list(shape), dt, offset=at)

    wsl = [alloc("ws%d" % i, [128, SLOT], BF16) for i in range(NS)]
    pp = alloc("pp", [128, NPP], F32)
    modt = alloc("modt", [128, 4, 48, 2], F32)
    drv = alloc("drv", [128, 4, 2, 8, 2], F32)
    gbt = alloc("gbt", [128, 2, 8, 2], F32)
    cact = alloc("cact", [128, 8, 2], BF16)
    csil = alloc("csil", [128, 8, 2], F32)
    ones = alloc("ones", [128, 128], BF16)
    ident = alloc("ident", [128, 128], BF16)
    cs_t = alloc("cs_t", [128, 2, 512], BF16)
    w1_t = alloc("w1_t", [128, 2, 256], BF16)
    base_ph = off[0]
    xt = alloc("xt", [128, 8, W], F32)
    ht = alloc("ht", [128, 8, W], BF16)
    hid_off = off[0]
    hid = alloc("hid", [128, 32, W], BF16)
    CH = W * 2
    vbuf = alloc("vbuf", [128, 8, W], F32, at=hid_off + 8 * CH)
    yzs = [alloc("yzs%d" % i, [128, 8, 2, 128], BF16, at=hid_off + 8 * i * CH) for i in range(4)]
    scr = [alloc("scr%d" % i, [128, W], F32) for i in range(4)]
    dg0_off = off[0]
    dg = [alloc("dg%d" % i, [128, 31, 128], BF16) for i in range(2)]
    dg1_off = dg0_off + 31 * 128 * 2
    Yt = alloc("Yt", [128, 128, 2, 128], BF16, at=base_ph)
    At = alloc("At", [128, 128, 256], BF16, at=base_ph + 65536)
    fTs = alloc("fTs", [128, 1, SS + 2 * H], BF16, at=base_ph)
    fTp = alloc("fTp", [128, 4, SP + 2 * H], BF16, at=base_ph)
    e2w = alloc("e2w", [128, 128, 2 * NKK], BF16, at=base_ph + 131072)
    fwc = [alloc("fwc%d" % i, [128, NKK * 128], BF16, at=base_ph + 140288 + 4608 * i) for i in range(2)]
    w1s_t = alloc("w1s_t", [128, 2, 256], BF16, at=base_ph + 149504)
    ps = nc.alloc_psum_tensor("ps", [128, 8, 512], F32)

    def ppc(name, i=0, n=1):
        o = PP[name] + i
        return pp[:, o:o + n]

    panels = []

    def k1024(w2d, c0):
        return (w2d[:, c0:c0 + 1024].rearrange("(k p) n -> p k n", p=128), (8, 1024))

    def layer_panels(i):
        kind, j = i % 3, i // 3
        pl = []
        if kind == 0:
            w = a_w_in.ap()[j]
            pl += [("win_c", k1024(w, 1024)), ("win_v", k1024(w, 2048)), ("win_b", k1024(w, 0)),
                   ("wout", k1024(a_w_out.ap()[j], 0))]
        elif kind == 1:
            pl += [("wout", k1024(b_w_out.ap(), 0))]
        else:
            pl += [("pw1a", k1024(c_w_pw1.ap(), 0)), ("pw1g", k1024(c_w_pw1.ap(), 1024)),
                   ("wout", k1024(c_w_pw2.ap(), 0))]
        return pl

    def mlp_panels(i):
        pl = [("up%d" % p, k1024(w_up.ap()[i], 1024 * p)) for p in range(4)]
        pl += [("dn%d" % p, (w_down.ap()[i][:, 256 * p:256 * p + 256].rearrange("(k p) n -> p k n", p=128), (32, 256)))
               for p in range(4)]
        return pl

    def e2_panels(v):
        return [("e2_%d" % s, (tab_e2[v].ap()[:, 32 * s:32 * s + 32, :, :].rearrange("p k c n -> p k (c n)"), (32, 256)))
                for s in range(4)]

    def deferred_ada(seg):
        return (2 + (seg - 1) // 6, (seg - 1) % 6) if 1 <= seg <= 12 else None

    for i in range(2):
        panels += [("ada%d" % p, k1024(ada_w.ap()[i], 1024 * p)) for p in range(6)]
    for seg in range(NSEGA):
        if deferred_ada(seg):
            di, dp = deferred_ada(seg)
            panels += [("ada%d" % dp, k1024(ada_w.ap()[di], 1024 * dp))]
        panels += layer_panels(0) + mlp_panels(0)
    panels += e2_panels(1) + e2_panels(1)
    for seg in range(NSEG):
        panels += layer_panels(1) + mlp_panels(1) + layer_panels(2) + mlp_panels(2) + layer_panels(3) + mlp_panels(3)

    pst = {"i": 0, "loaded": 0}

    def next_panel(tag):
        i = pst["i"]
        pst["i"] += 1
        assert panels[i][0] == tag, (i, panels[i][0], tag)
        while pst["loaded"] < min(len(panels), i + NS):
            j = pst["loaded"]
            src, (a, b) = panels[j][1]
            sl = j % NS
            dst = wsl[sl][:, 0:a * b].rearrange("p (a b) -> p a b", a=a)
            P.dma("pool", "w%d" % sl, dst, src, reads=[], writes=[("ws", sl)])
            pst["loaded"] += 1
        sl = i % NS
        a, b = panels[i][1][1]
        return sl, wsl[sl][:, 0:a * b].rearrange("p (a b) -> p a b", a=a)

    gst = {"g": 0}
    cur = {"NT": NT}

    def mm_group(nw, pairs, reads):
        bank = gst["g"] % 8
        gst["g"] += 1
        n = len(pairs)
        fns = [lambda e, l=l, r=r, i=i: e.matmul(ps[:, bank, 0:nw], lhsT=l, rhs=r, start=(i == 0), stop=(i == n - 1))
               for i, (l, r) in enumerate(pairs)]
        P.ops("pe", fns, reads=reads, writes=[("ps", bank)])
        return bank

    def linear(tag, rhs, rhs_keys, evac, kc=8, mper=8, mcols=128, n_major=False, tile_hook=None, hook_delay=3):
        sl, wv = next_panel(tag)
        P.stage = tag
        rk = rhs_keys if callable(rhs_keys) else (lambda ni: rhs_keys)
        if tile_hook is not None:
            n_major = True
        NTc = cur["NT"]
        order = [(m, ni) for ni in range(len(NTc)) for m in range(mper)] if n_major else [(m, ni) for m in range(mper) for ni in range(len(NTc))]
        fire = {}
        if tile_hook is not None:
            for ni in range(len(NTc)):
                fire.setdefault(min((ni + 1) * mper - 1 + hook_delay, len(order) - 1), []).append(ni)
        for gi, (m, ni) in enumerate(order):
            n0, n1 = NTc[ni]
            pairs = [(wv[:, k, m * mcols:(m + 1) * mcols], rhs(k, n0, n1)) for k in range(kc)]
            bank = mm_group(n1 - n0, pairs, reads=[("ws", sl)] + rk(ni))
            evac(m, ni, n0, n1, bank)
            for nj in fire.get(gi, []):
                tile_hook(nj)
                P.stage = tag

    P.dma("sp", "d_pp", pp[:], ppd.ap(), writes=[("pp",)])
    P.dma("pool", "d_cs", cs_t[:], tab_cs.ap().rearrange("(k p) n -> p k n", p=128), writes=[("cs",)])
    P.dma("pool", "d_w1", w1_t[:], tab_w1.ap().rearrange("p (k n) -> p k n", k=2), writes=[("w1",)])
    P.dma("pool", "d_id", ident[:], tab_id.ap(), writes=[("ident",)])
    P.op("dve", lambda e: e.memset(ones[:], 1.0), writes=[("ones",)])
    cv = pp[:, PP["c"]:PP["c"] + 16].rearrange("p (k c) -> p k c", k=8)
    P.op("act", lambda e: e.activation(out=csil[:], in_=cv, func=AF.Silu), reads=[("pp",)], writes=[("csil",)])
    P.op("act", lambda e: e.activation(out=cact[:], in_=csil[:], func=AF.Copy), reads=[("csil",)], writes=[("cact",)])
    def ada_panel(i, p):
        P.stage = "ada"
        sl, wv = next_panel("ada%d" % p)
        for m in range(8):
            mg = 8 * p + m
            pairs = [(wv[:, k, m * 128:(m + 1) * 128], cact[:, k, :]) for k in range(8)]
            bank = mm_group(2, pairs, reads=[("ws", sl), ("cact",)])
            P.op("dve", lambda e, mg=mg, bank=bank: e.tensor_scalar(
                out=modt[:, i, mg, :], in0=ps[:, bank, 0:2], scalar1=ppc("ada_b", i * 48 + mg), scalar2=None, op0=ALU.add),
                reads=[("ps", bank), ("pp",)], writes=[("mod",)])

    def ada_derived(i):
        for n, (gname, sc0) in enumerate((("nmix", 8), ("nmlp", 32))):
            for col in range(2):
                P.op("dve", lambda e, n=n, col=col, sc0=sc0: e.tensor_scalar(
                    out=drv[:, i, n, :, col], in0=modt[:, i, sc0:sc0 + 8, col], scalar1=1.0, scalar2=None, op0=ALU.add),
                    reads=[("mod",)], writes=[("drv",)])
                P.op("dve", lambda e, n=n, col=col, gname=gname: e.tensor_tensor(
                    out=drv[:, i, n, :, col], in0=drv[:, i, n, :, col], in1=pp[:, PP[gname] + 8 * i:PP[gname] + 8 * i + 8], op=ALU.mult),
                    reads=[("drv",), ("pp",)], writes=[("drv",)])
        for n, (bname, li) in enumerate((("bb", 1), ("cb2", 2))):
            if li != i:
                continue
            for col in range(2):
                P.op("dve", lambda e, n=n, col=col, bname=bname, li=li: e.tensor_tensor(
                    out=gbt[:, n, :, col], in0=modt[:, li, 16:24, col], in1=pp[:, PP[bname]:PP[bname] + 8], op=ALU.mult),
                    reads=[("mod",), ("pp",)], writes=[("gb",)])

    for i in range(2):
        for p in range(6):
            ada_panel(i, p)
        ada_derived(i)

    def mod(i, which, k, col):
        return modt[:, i, which * 8 + k, col:col + 1]

    CONST = [("pp",), ("mod",), ("drv",), ("gb",), ("ones",)]
    ALLK = [("x", k) for k in range(8)] + [("h", k, ni) for k in range(8) for ni in range(3)] + [("rstd", ni) for ni in range(3)] + [("hid", k) for k in range(32)] + [("scr", k) for k in range(4)]
    EPSAP = pp[:, PP["eps"]:PP["eps"] + 1]
    ZEROAP = pp[:, PP["zero"]:PP["zero"] + 1]

    def halo_view(t3, k):
        a = t3[:, k, 0:H]
        return bass.AP(a.tensor, a.offset, [list(a.ap[0]), [W - H, 2], [1, H]])

    def hkf(ni):
        return [("h", k, ni) for k in range(8)]
    HK_ALL = [("h", k, ni) for k in range(8) for ni in range(3)]

    def sq_spec(which):
        if which == "h":
            return (lambda k: ht[:, k, :]), (lambda k, ni: ("h", k, ni))
        return (lambda k: hid[:, 24 + k, :]), (lambda k, ni: ("hid", 24 + k))

    def emit_square(which, k, ni, n0, n1, eng="act"):
        tfn, kfn = sq_spec(which)
        if eng == "act":
            P.op("act", lambda e: e.activation(out=tfn(k)[:, n0:n1], in_=xt[:, k, n0:n1], func=AF.Square),
                 reads=[("x", k)], writes=[kfn(k, ni)])
        else:
            P.op("dve", lambda e: e.tensor_tensor(out=tfn(k)[:, n0:n1], in0=xt[:, k, n0:n1], in1=xt[:, k, n0:n1], op=ALU.mult),
                 reads=[("x", k)], writes=[kfn(k, ni)])

    def norm_tile(which, ni, act_fn, out_keys):
        P.stage = "norm"
        rstd = scr[0]
        tfn, kfn = sq_spec(which)
        n0, n1 = cur["NT"][ni]
        pairs = [(ones[:, :], tfn(k)[:, n0:n1]) for k in range(8)]
        bank = mm_group(n1 - n0, pairs, reads=[("ones",)] + [kfn(k, ni) for k in range(8)])
        P.op("act", lambda e: e.activation(out=rstd[:, n0:n1], in_=ps[:, bank, 0:n1 - n0], func=AF.Sqrt, bias=EPSAP, scale=1.0 / D),
             reads=[("ps", bank), ("pp",)], writes=[("rstd", ni)])
        P.op("dve", lambda e: e.reciprocal(out=rstd[:, n0:n1], in_=rstd[:, n0:n1]), reads=[("rstd", ni)], writes=[("rstd", ni)])
        for k in range(8):
            tb = 1 + (k % 2)
            P.op("dve", lambda e, k=k, tb=tb: e.tensor_tensor(out=scr[tb][:, n0:n1], in0=xt[:, k, n0:n1], in1=rstd[:, n0:n1], op=ALU.mult),
                 reads=[("x", k), ("rstd", ni)], writes=[("scr", tb)])
            P.op("act", act_fn(k, n0, n1, scr[tb]), reads=[("scr", tb)] + CONST, writes=out_keys(k, ni))

    def norm_apply(which, presq, act_fn, out_keys):
        if not presq:
            for ni, (n0, n1) in enumerate(cur["NT"]):
                for k in range(8):
                    emit_square(which, k, ni, n0, n1, eng=("act" if k % 2 == 0 else "dve"))
        for ni in range(len(cur["NT"])):
            norm_tile(which, ni, act_fn, out_keys)

    def norm_mod_fns(i, n, col):
        return ((lambda k, n0, n1, tbuf: (lambda e: e.activation(out=ht[:, k, n0:n1], in_=tbuf[:, n0:n1], func=AF.Identity,
                                                                 bias=mod(i, 3 * n, k, col), scale=drv[:, i, n, k, col:col + 1]))),
                (lambda k, ni: [("h", k, ni)]))

    def norm_mod(i, n, col, which="h", presq=False):
        a, o = norm_mod_fns(i, n, col)
        norm_apply(which, presq, a, o)

    def norm_mod_hook(i, n, col, which="h"):
        a, o = norm_mod_fns(i, n, col)
        return lambda ni: norm_tile(which, ni, a, o)

    def resid_evac(i, gidx, col, sqw="h"):
        def ev(m, ni, n0, n1, bank, moff=0):
            mm = m + moff
            P.op("dve", lambda e: e.scalar_tensor_tensor(out=xt[:, mm, n0:n1], in0=ps[:, bank, 0:n1 - n0], scalar=mod(i, gidx, mm, col),
                                                         in1=xt[:, mm, n0:n1], op0=ALU.mult, op1=ALU.add),
                 reads=[("ps", bank), ("x", mm)] + CONST, writes=[("x", mm)])
            emit_square(sqw, mm, ni, n0, n1)
        return ev

    def resid_bias_evac(i, gbn, col, sqw="h"):
        def ev(m, ni, n0, n1, bank):
            tb = 1 + (gst["g"] % 3)
            P.op("act", lambda e: e.activation(out=scr[tb][:, 0:n1 - n0], in_=ps[:, bank, 0:n1 - n0], func=AF.Identity,
                                               bias=gbt[:, gbn, m, col:col + 1], scale=mod(i, 2, m, col)),
                 reads=[("ps", bank)] + CONST, writes=[("scr", tb)])
            P.op("dve", lambda e: e.tensor_tensor(out=xt[:, m, n0:n1], in0=xt[:, m, n0:n1], in1=scr[tb][:, 0:n1 - n0], op=ALU.add),
                 reads=[("scr", tb), ("x", m)], writes=[("x", m)])
            emit_square(sqw, m, ni, n0, n1)
        return ev

    def mlp(i, col, which="h", prenormed=False, next_hook=None):
        if not prenormed:
            norm_mod(i, 1, col, which=which, presq=True)
        hk = hkf
        for p in range(4):
            def ev(m, ni, n0, n1, bank, p=p):
                mg = 8 * p + m
                tb = 1 + (gst["g"] % 3)
                P.op("act", lambda e: e.activation(out=scr[tb][:, 0:n1 - n0], in_=ps[:, bank, 0:n1 - n0], func=AF.Relu),
                     reads=[("ps", bank)], writes=[("scr", tb)])
                P.op("dve", lambda e: e.tensor_tensor(out=hid[:, mg, n0:n1], in0=scr[tb][:, 0:n1 - n0], in1=scr[tb][:, 0:n1 - n0], op=ALU.mult),
                     reads=[("scr", tb)], writes=[("hid", mg)])
            linear("up%d" % p, lambda k, n0, n1: ht[:, k, n0:n1], hk, ev, n_major=(p == 0))
        hidk = [("hid", k) for k in range(32)]
        for p in range(4):
            rev = resid_evac(i, 5, col)
            linear("dn%d" % p, lambda k, n0, n1: hid[:, k, n0:n1], hidk,
                   lambda m, ni, n0, n1, bank, p=p: rev(m, ni, n0, n1, bank, moff=2 * p), kc=32, mper=2, mcols=128,
                   tile_hook=(next_hook if p == 3 else None), hook_delay=1)

    def short_conv(i, col, seg, presq, prenormed=False, next_hook=None, wout_nt=None):
        j = i // 3
        if not prenormed:
            norm_mod(i, 0, col, presq=presq)
        hk = hkf
        rhs = lambda k, n0, n1: ht[:, k, n0:n1]
        linear("win_c", rhs, hk, lambda m, ni, n0, n1, bank: P.op(
            "act", lambda e: e.activation(out=vbuf[:, m, n0:n1], in_=ps[:, bank, 0:n1 - n0], func=AF.Copy),
            reads=[("ps", bank)], writes=[("hid", 8 + 2 * m), ("hid", 9 + 2 * m)]), n_major=True)

        def ev_v(m, ni, n0, n1, bank):
            P.op("dve", lambda e: e.tensor_tensor(out=vbuf[:, m, n0:n1], in0=ps[:, bank, 0:n1 - n0], in1=vbuf[:, m, n0:n1], op=ALU.mult),
                 reads=[("ps", bank), ("hid", 8 + 2 * m), ("hid", 9 + 2 * m)], writes=[("hid", 8 + 2 * m), ("hid", 9 + 2 * m)])
            if ni == 2:
                mk = pp[:, PP["mask"] + seg * 2 * H:PP["mask"] + (seg + 1) * 2 * H].rearrange("p (a b) -> p a b", a=2)
                P.op("dve", lambda e: e.tensor_tensor(out=halo_view(vbuf, m), in0=halo_view(vbuf, m), in1=mk, op=ALU.mult),
                     reads=[("hid", 8 + 2 * m), ("hid", 9 + 2 * m), ("pp",)], writes=[("hid", 8 + 2 * m), ("hid", 9 + 2 * m)])
        linear("win_v", rhs, hk, ev_v)

        def cw(tap, m):
            return ppc("aconv", (j * 3 + tap) * 8 + m)

        def ev_b(m, ni, n0, n1, bank):
            tb = 1 + (m % 2)
            if ni == 0:
                vk = [("hid", 8 + 2 * m), ("hid", 9 + 2 * m)]
                P.op("act", lambda e: e.activation(out=scr[tb][:, 1:W - 1], in_=vbuf[:, m, 1:W - 1], func=AF.Identity, bias=ZEROAP, scale=cw(1, m)),
                     reads=vk + [("pp",)], writes=[("scr", tb)])
                P.op("dve", lambda e: e.scalar_tensor_tensor(out=scr[tb][:, 1:W - 1], in0=vbuf[:, m, 0:W - 2], scalar=cw(0, m),
                                                             in1=scr[tb][:, 1:W - 1], op0=ALU.mult, op1=ALU.add),
                     reads=vk + [("pp",), ("scr", tb)], writes=[("scr", tb)])
                P.op("dve", lambda e: e.scalar_tensor_tensor(out=scr[tb][:, 1:W - 1], in0=vbuf[:, m, 2:W], scalar=cw(2, m),
                                                             in1=scr[tb][:, 1:W - 1], op0=ALU.mult, op1=ALU.add),
                     reads=vk + [("pp",), ("scr", tb)], writes=[("scr", tb)])
                P.op("dve", lambda e: e.tensor_copy(out=halo_view_1(scr[tb]), in_=halo_view_1v(vbuf, m)),
                     reads=vk + [("scr", tb)], writes=[("scr", tb)])
            P.op("dve", lambda e: e.tensor_tensor(out=hid[:, m, n0:n1], in0=ps[:, bank, 0:n1 - n0], in1=scr[tb][:, n0:n1], op=ALU.mult),
                 reads=[("ps", bank), ("scr", tb)], writes=[("hid", m)])
        linear("win_b", rhs, hk, ev_b)
        if wout_nt is not None:
            cur["NT"] = wout_nt
        linear("wout", lambda k, n0, n1: hid[:, k, n0:n1], [("hid", k) for k in range(8)], resid_evac(i, 2, col), tile_hook=next_hook)

    def halo_view_1(t2):
        a = t2[:, 0:1]
        return bass.AP(a.tensor, a.offset, [list(a.ap[0]), [W - 1, 2]])

    def halo_view_1v(t3, k):
        a = t3[:, k, 0:1]
        return bass.AP(a.tensor, a.offset, [list(a.ap[0]), [W - 1, 2]])

    def conformer(i, col, seg, prenormed=False, next_hook=None):
        if not prenormed:
            norm_mod(i, 0, col, presq=True)
        hk = hkf
        rhs = lambda k, n0, n1: ht[:, k, n0:n1]
        vkeys = lambda m: [("hid", 8 + 2 * m), ("hid", 9 + 2 * m)]
        linear("pw1a", rhs, hk, lambda m, ni, n0, n1, bank: P.op(
            "act", lambda e: e.activation(out=vbuf[:, m, n0:n1], in_=ps[:, bank, 0:n1 - n0], func=AF.Identity, bias=ppc("cb1", m), scale=1.0),
            reads=[("ps", bank), ("pp",)], writes=vkeys(m)), n_major=True)

        def ev_g(m, ni, n0, n1, bank):
            tb = 1 + (gst["g"] % 3)
            P.op("act", lambda e: e.activation(out=scr[tb][:, 0:n1 - n0], in_=ps[:, bank, 0:n1 - n0], func=AF.Sigmoid, bias=ppc("cb1", 8 + m), scale=1.0),
                 reads=[("ps", bank), ("pp",)], writes=[("scr", tb)])
            P.op("dve", lambda e: e.tensor_tensor(out=hid[:, m, n0:n1], in0=vbuf[:, m, n0:n1], in1=scr[tb][:, 0:n1 - n0], op=ALU.mult),
                 reads=vkeys(m) + [("scr", tb)], writes=[("hid", m)])
            if ni == 2:
                mk = pp[:, PP["mask"] + seg * 2 * H:PP["mask"] + (seg + 1) * 2 * H].rearrange("p (a b) -> p a b", a=2)
                P.op("dve", lambda e: e.tensor_tensor(out=halo_view(hid, m), in0=halo_view(hid, m), in1=mk, op=ALU.mult),
                     reads=[("hid", m), ("pp",)], writes=[("hid", m)])
        linear("pw1g", rhs, hk, ev_g)
        CT = [(15, 358), (358, 701), (701, W - 15)]
        for m in range(8):
            d = dg[m % 2]
            P.ops("dve", [lambda e, tap=tap, d=d, m=m: e.tensor_scalar(out=d[:, tap, :], in0=ident[:, :], scalar1=ppc("cdw", tap * 8 + m), scalar2=None, op0=ALU.mult)
                           for tap in range(31)], reads=[("ident",), ("pp",), ("x", 0)], writes=[("dg", m % 2)])
            for (n0, n1) in CT:
                pairs = [(d[:, tap, :], hid[:, m, n0 + tap - 15:n1 + tap - 15]) for tap in range(31)]
                bank = mm_group(n1 - n0, pairs, reads=[("dg", m % 2), ("hid", m)])
                P.op("act", lambda e, m=m, n0=n0, n1=n1, bank=bank: e.activation(out=vbuf[:, m, n0:n1], in_=ps[:, bank, 0:n1 - n0], func=AF.Identity, bias=ppc("cdb", m), scale=1.0),
                     reads=[("ps", bank), ("pp",)], writes=vkeys(m))
        cur["NT"] = NT_TRIM
        mu, rs = scr[0], scr[1]
        for ni, (n0, n1) in enumerate(cur["NT"]):
            for k in range(8):
                P.op("act", lambda e, k=k, n0=n0, n1=n1: e.activation(out=ht[:, k, n0:n1], in_=vbuf[:, k, n0:n1], func=AF.Copy),
                     reads=vkeys(k), writes=[("h", k, ni)])
                P.op("act", lambda e, k=k, n0=n0, n1=n1: e.activation(out=hid[:, 24 + k, n0:n1], in_=vbuf[:, k, n0:n1], func=AF.Square),
                     reads=vkeys(k), writes=[("hid", 24 + k)])
            b1 = mm_group(n1 - n0, [(ones[:, :], ht[:, k, n0:n1]) for k in range(8)], reads=[("ones",)] + hkf(ni))
            b2 = mm_group(n1 - n0, [(ones[:, :], hid[:, 24 + k, n0:n1]) for k in range(8)], reads=[("ones",)] + [("hid", 24 + k) for k in range(8)])
            nw = n1 - n0
            P.op("act", lambda e, n0=n0, n1=n1, b1=b1, nw=nw: e.activation(out=mu[:, n0:n1], in_=ps[:, b1, 0:nw], func=AF.Identity, bias=ZEROAP, scale=1.0 / D),
                 reads=[("ps", b1)], writes=[("scr", 0)])
            P.op("dve", lambda e, n0=n0, n1=n1: e.tensor_tensor(out=scr[2][:, n0:n1], in0=mu[:, n0:n1], in1=mu[:, n0:n1], op=ALU.mult),
                 reads=[("scr", 0)], writes=[("scr", 2)])
            P.op("dve", lambda e, n0=n0, n1=n1, b2=b2, nw=nw: e.scalar_tensor_tensor(out=rs[:, n0:n1], in0=ps[:, b2, 0:nw], scalar=1.0 / D, in1=scr[2][:, n0:n1],
                                                                              op0=ALU.mult, op1=ALU.subtract),
                 reads=[("ps", b2), ("scr", 2)], writes=[("scr", 1)])
        P.op("dve", lambda e: e.tensor_scalar(out=rs[:, :], in0=rs[:, :], scalar1=0.0, scalar2=None, op0=ALU.max), reads=[("scr", 1)], writes=[("scr", 1)])
        P.op("act", lambda e: e.activation(out=rs[:, :], in_=rs[:, :], func=AF.Sqrt, bias=EPSAP, scale=1.0), reads=[("scr", 1), ("pp",)], writes=[("scr", 1)])
        P.op("dve", lambda e: e.reciprocal(out=rs[:, :], in_=rs[:, :]), reads=[("scr", 1)], writes=[("scr", 1)])
        P.op("dve", lambda e: e.scalar_tensor_tensor(out=mu[:, :], in0=mu[:, :], scalar=-1.0, in1=rs[:, :], op0=ALU.mult, op1=ALU.mult),
             reads=[("scr", 0), ("scr", 1)], writes=[("scr", 0)])
        for k in range(8):
            tb = 2 + (k % 2)
            P.op("dve", lambda e, k=k, tb=tb: e.tensor_tensor(out=scr[tb][:, :], in0=vbuf[:, k, :], in1=rs[:, :], op=ALU.mult),
                 reads=vkeys(k) + [("scr", 1)], writes=[("scr", tb)])
            P.op("dve", lambda e, k=k, tb=tb: e.tensor_tensor(out=scr[tb][:, :], in0=scr[tb][:, :], in1=mu[:, :], op=ALU.add),
                 reads=[("scr", 0), ("scr", tb)], writes=[("scr", tb)])
            P.op("act", lambda e, k=k, tb=tb: e.activation(out=ht[:, k, :], in_=scr[tb][:, :], func=AF.Silu, bias=ppc("lnb", k), scale=ppc("lng", k)),
                 reads=[("scr", tb), ("pp",)], writes=[("h", k, n3) for n3 in range(3)])
        linear("wout", rhs, hk, resid_bias_evac(i, 1, col, sqw="hid"), tile_hook=next_hook)

    def seg_col(seg):
        return 0 if 2 <= seg < 6 else 1

    def load_x_a(seg):
        P.dma("sp", "d_x", xt[:], xs.ap()[seg].rearrange("(k p) w -> p k w", p=128),
              writes=[("x", k) for k in range(8)])

    load_x_a(0)
    for seg in range(NSEGA):
        col = seg_col(seg)
        cur["NT"] = NT_TRIM if seg >= NSEG else NT
        if deferred_ada(seg):
            di, dp = deferred_ada(seg)
            ada_panel(di, dp)
            if dp == 5:
                ada_derived(di)
        short_conv(0, col, seg, presq=False, next_hook=norm_mod_hook(0, 1, col))
        mlp(0, col, prenormed=True, next_hook=norm_mod_hook(1, 0, col))
        if seg < NSEG:
            P.dma("sp", "d_xst", xd.ap()[seg].rearrange("(k p) w -> p k w", p=128), xt[:],
                  reads=[("x", k) for k in range(8)], writes=[("xd", seg)])
        if seg + 1 < NSEGA:
            load_x_a(seg + 1)
        for tb in range(8):
            c0 = H + 128 * tb
            yb = yzs[tb % 4]
            ykeys = [("hid", 8 * (tb % 4)), ("hid", 1 + 8 * (tb % 4))]
            for g in range(4):
                pairs = [(ht[:, 2 * g + kk, c0:c0 + 128], cs_t[:, kk, :]) for kk in range(2)]
                bank = mm_group(512, pairs, reads=[("cs",)] + [("h", 2 * g + kk, n3) for kk in range(2) for n3 in range(3)])
                eng = "act" if g % 2 == 0 else "dve"
                outv = yb[:, 2 * g:2 * g + 2, :, :].rearrange("p c y q -> p y c q")
                inv = ps[:, bank, :].rearrange("p (y c q) -> p y c q", y=2, c=2)
                if eng == "act":
                    P.op("act", lambda e, outv=outv, inv=inv: e.activation(out=outv, in_=inv, func=AF.Copy), reads=[("ps", bank)], writes=ykeys)
                else:
                    P.op("dve", lambda e, outv=outv, inv=inv: e.tensor_copy(out=outv, in_=inv), reads=[("ps", bank)], writes=ykeys)
            if not (2 <= seg < 6):
                ps_i = seg if seg < 2 else seg - 4
                tt0 = ps_i * T + 128 * tb
                dst = yzsd.ap().rearrange("c t e -> t c e")[tt0:tt0 + 128, :, :]
                P.dma("sp", "d_yz%d" % (tb % 4), dst, yb[:].rearrange("p c y q -> p c (y q)"), reads=ykeys, writes=[("yzsd",)])
            else:
                t0 = (seg - 2) * T + 128 * tb
                P.dma("sp", "d_yz%d" % (tb % 4), yzp.ap()[t0:t0 + 128, :, :], yb[:].rearrange("p c y q -> p c (y q)"), reads=ykeys, writes=[("yzp",)])

    cur["NT"] = NT
    YK = ("phB", 0)
    AK = ("phB", 1)
    P.dma("pool", "d_w1s", w1s_t[:], tab_w1s.ap().rearrange("p (k n) -> p k n", k=2), writes=[("w1s",)] + ALLK)
    P.dma("pool", "d_e2w", e2w[:], tab_e2w.ap(), writes=[("e2w",)] + ALLK)

    def stage1(w1tile, w1key):
        for q2 in range(64):
            bank = gst["g"] % 8
            gst["g"] += 1
            fns = []
            for qq in range(2):
                q = 2 * q2 + qq
                o = ps[:, bank, 256 * qq:256 * qq + 256]
                fns.append(lambda e, o=o, q=q: e.matmul(o, lhsT=Yt[:, :, 0, q], rhs=w1tile[:, 0, :], start=True, stop=False))
                fns.append(lambda e, o=o, q=q: e.matmul(o, lhsT=Yt[:, :, 1, q], rhs=w1tile[:, 1, :], start=False, stop=True))
            P.ops("pe", fns, reads=[YK, w1key], writes=[("ps", bank)])
            outv = At[:, 2 * q2:2 * q2 + 2, :]
            inv = ps[:, bank, :].rearrange("p (q n) -> p q n", q=2)
            if q2 % 2 == 0:
                P.op("act", lambda e, outv=outv, inv=inv: e.activation(out=outv, in_=inv, func=AF.Copy), reads=[("ps", bank)], writes=[AK])
            else:
                P.op("dve", lambda e, outv=outv, inv=inv: e.tensor_copy(out=outv, in_=inv), reads=[("ps", bank)], writes=[AK])

    def stage2_full(S, nb, c4n, fT):
        Sp = S + 2 * H
        P.op("dve", lambda e: e.memset(fT[:, :, 0:H], 0.0), reads=[], writes=[YK])
        P.op("dve", lambda e: e.memset(fT[:, :, H + S:Sp], 0.0), reads=[], writes=[YK])
        for sl4 in range(4):
            sl, ev = next_panel("e2_%d" % sl4)
            for kb in range(8):
                bank = gst["g"] % 8
                gst["g"] += 1
                fns = []
                for jj in range(4):
                    kl = 4 * kb + jj
                    k1 = 32 * sl4 + kl
                    o = ps[:, bank, 128 * jj:128 * jj + 128]
                    fns.append(lambda e, o=o, k1=k1, kl=kl, ev=ev: e.matmul(o, lhsT=At[:, :, k1], rhs=ev[:, kl, 0:128], start=True, stop=False))
                    fns.append(lambda e, o=o, k1=k1, kl=kl, ev=ev: e.matmul(o, lhsT=At[:, :, 128 + k1], rhs=ev[:, kl, 128:256], start=False, stop=True))
                P.ops("pe", fns, reads=[AK, ("ws", sl)], writes=[("ps", bank)])
                k1b = 32 * sl4 + 4 * kb
                a = fT[:, 0, H + k1b:H + k1b + 1]
                outv = bass.AP(a.tensor, a.offset, [list(a.ap[0]), [1, 4], [Sp, c4n], [128, nb]])
                inv = ps[:, bank, :].rearrange("p (j c k) -> p j c k", j=4, c=c4n)
                if kb % 2 == 0:
                    P.op("act", lambda e, outv=outv, inv=inv: e.activation(out=outv, in_=inv, func=AF.Copy), reads=[("ps", bank)], writes=[YK])
                else:
                    P.op("dve", lambda e, outv=outv, inv=inv: e.tensor_copy(out=outv, in_=inv), reads=[("ps", bank)], writes=[YK])

    def stage2_win(i):
        fw = fwc[i % 2]
        fk = ("fwc", i % 2)
        for g16 in range(8):
            bank = gst["g"] % 8
            gst["g"] += 1
            fns = []
            for jj in range(16):
                k1 = 16 * g16 + jj
                o = ps[:, bank, NKK * jj:NKK * jj + NKK]
                fns.append(lambda e, o=o, k1=k1: e.matmul(o, lhsT=At[:, :, k1], rhs=e2w[:, k1, 0:NKK], start=True, stop=False))
                fns.append(lambda e, o=o, k1=k1: e.matmul(o, lhsT=At[:, :, 128 + k1], rhs=e2w[:, k1, NKK:2 * NKK], start=False, stop=True))
            P.ops("pe", fns, reads=[AK, ("e2w",)], writes=[("ps", bank)])
            a = fw[:, 16 * g16:16 * g16 + 1]
            outv = bass.AP(a.tensor, a.offset, [list(a.ap[0]), [1, 16], [128, NKK]])
            inv = ps[:, bank, 0:16 * NKK].rearrange("p (j k) -> p j k", j=16)
            if g16 % 2 == 0:
                P.op("act", lambda e, outv=outv, inv=inv: e.activation(out=outv, in_=inv, func=AF.Copy), reads=[("ps", bank)], writes=[fk])
            else:
                P.op("dve", lambda e, outv=outv, inv=inv: e.tensor_copy(out=outv, in_=inv), reads=[("ps", bank)], writes=[fk])
        P.dma("sp", "d_fsd%d" % (i % 2), fsd.ap()[i], fw[:, :], reads=[fk], writes=[("fsd",)])

    for hh in range(2):
        src = yzp.ap().rearrange("(a b) c e -> a b c e", b=32)
        for c4 in range(4):
            dstv = Yt[:, 32 * c4:32 * c4 + 32, :, :].rearrange("p b y q -> p b (y q)")
            P.dma("sp", "d_ytl", dstv, src[:, :, 4 * hh + c4, :], reads=[("yzp",)], writes=[YK] + (ALLK if (hh == 0 and c4 == 0) else []))
        stage1(w1_t, ("w1",))
        stage2_full(SP, 32, 4, fTp)
        for c4 in range(4):
            P.dma("sp", "d_fp", fpd.ap()[4 * hh + c4], fTp[:, c4, :], reads=[YK], writes=[("fpd",)])
    def load_yt_sample(i):
        srcv = yzsd.ap()[i].rearrange("(a b) e -> a (b e)", b=128)
        P.dma("sp", "d_ytl", Yt[:, :, :, :].rearrange("p b y q -> p (b y q)"), srcv, reads=[("yzsd",)], writes=[YK])

    load_yt_sample(0)
    for i in range(8):
        stage1(w1s_t, ("w1s",))
        if i + 1 < 8:
            load_yt_sample(i + 1)
        stage2_win(i)

    order = [2, 3, 4, 5, 0, 1]
    PHBK = [("phB", 0), ("phB", 1), ("e2w",), ("w1s",), ("fwc", 0), ("fwc", 1)]
    def load_c(seg):
        P.dma("sp", "d_x", xt[:], xd.ap()[seg].rearrange("(k p) w -> p k w", p=128),
              reads=[("xd", seg)], writes=[("x", k) for k in range(8)] + PHBK)
        if seg >= 2:
            c0 = (seg - 2) * T
            fsrc = fpd.ap()[:, :, c0:c0 + W].rearrange("k p w -> p k w")
            fk = ("fpd",)
        else:
            c0 = seg * T
            fsrc = fsd.ap()[:, :, FOFF + c0:FOFF + c0 + W].rearrange("k p w -> p k w")
            fk = ("fsd",)
        P.dma("sp", "d_f", hid[:, 0:8, :], fsrc, reads=[fk], writes=[("hid", k) for k in range(8)] + PHBK)

    load_c(order[0])
    for oi, seg in enumerate(order):
        col = seg_col(seg)
        cur["NT"] = NT
        linear("wout", lambda k, n0, n1: hid[:, k, n0:n1], [("hid", k) for k in range(8)], resid_bias_evac(1, 0, col),
               tile_hook=norm_mod_hook(1, 1, col))
        mlp(1, col, prenormed=True, next_hook=norm_mod_hook(2, 0, col))
        conformer(2, col, seg, prenormed=True, next_hook=norm_mod_hook(2, 1, col, which="hid"))
        mlp(2, col, which="hid", prenormed=True, next_hook=norm_mod_hook(3, 0, col))
        short_conv(3, col, seg, presq=True, prenormed=True, next_hook=norm_mod_hook(3, 1, col), wout_nt=NT_OWN)
        mlp(3, col, prenormed=True)
        norm_apply("h", True,
                   lambda k, n0, n1, tbuf: (lambda e: e.activation(out=vbuf[:, k, n0:n1], in_=tbuf[:, n0:n1], func=AF.Identity, bias=ZEROAP, scale=ppc("fin", k))),
                   lambda k, ni: [("hid", 8 + 2 * k), ("hid", 9 + 2 * k)])
        if oi + 1 < len(order):
            load_c(order[oi + 1])
        P.dma("sp", "d_out", yout.ap()[:, seg * T:(seg + 1) * T].rearrange("(k p) t -> p k t", p=128), vbuf[:, :, H:H + T],
              reads=[("hid", 8 + kk) for kk in range(16)], writes=[("yout",)])
    assert pst["i"] == len(panels), (pst["i"], len(panels))
    P.final_wait("sp")

    with ExitStack() as es:
        for name in sorted(P.semnames):
            P.sems[name] = es.enter_context(nc.semaphore(name))
        block = es.enter_context(nc.Block())

        @block.tensor
        def _(e):
            for f in P.q["pe"]:
                f(e)

        @block.scalar
        def _(e):
            for f in P.q["act"]:
                f(e)

        @block.vector
        def _(e):
            for f in P.q["dve"]:
                f(e)

        @block.gpsimd
        def _(e):
            for f in P.q["pool"]:
                f(e)

        @block.sync
        def _(e):
            for f in P.q["sp"]:
                f(e)
    return nc


def _vec(v, n):
    return np.ascontiguousarray(np.asarray(v, np.float32).reshape(n, 128).T)


def _tables():
    a = np.arange(128)
    ang1 = 2 * np.pi * np.outer(a, a) / 128
    C1, S1 = np.cos(ang1), np.sin(ang1)
    w1 = np.concatenate([C1, S1, -S1, C1], 1).astype(np.float32)
    c = np.arange(256)
    angc = 2 * np.pi * np.outer(c, c) / 256
    cs = (np.concatenate([np.cos(angc), np.sin(angc)], 1) / 16.0).astype(np.float32)

    def e2(S):
        nb = S // 128
        c4n = 128 // nb
        b = np.arange(nb)
        k = np.arange(128)[:, None] + 128 * np.arange(nb)[None, :]
        ang = 2 * np.pi * b[:, None, None] * k[None] / S
        sc = 1.0 / np.sqrt(S)
        E = np.zeros((c4n, nb, 128, 2, c4n, nb), np.float32)
        for c4 in range(c4n):
            E[c4, :, :, 0, c4, :] = np.cos(ang) * sc
            E[c4, :, :, 1, c4, :] = -np.sin(ang) * sc
        return E.reshape(128, 128, 2, 128)
    return cs, w1, e2(SP)


_CACHE = {}


def kernel(x_prompt, x_sample, c_prompt, c_sample, ada_w, ada_b, norm_mix, norm_mlp,
           a_w_in, a_conv_w, a_w_out, b_w_out, b_b_out,
           c_w_pw1, c_b_pw1, c_dw_w, c_dw_b, c_ln_g, c_ln_b, c_w_pw2, c_b_pw2,
           mlp_w_up, mlp_w_down, final_norm):
    f32 = lambda a: np.ascontiguousarray(np.asarray(a, np.float32))
    x_prompt, x_sample = f32(x_prompt), f32(x_sample)
    if "nc" not in _CACHE:
        _CACHE["nc"] = build_program()
        _CACHE["tabs"] = _tables()
    nc = _CACHE["nc"]
    cs, w1, e2p = _CACHE["tabs"]
    shared = {
        "ada_w": f32(ada_w), "a_w_in": f32(a_w_in), "a_w_out": f32(a_w_out), "b_w_out": f32(b_w_out)[0],
        "c_w_pw1": f32(c_w_pw1)[0], "c_w_pw2": f32(c_w_pw2)[0], "mlp_w_up": f32(mlp_w_up), "mlp_w_down": f32(mlp_w_down),
        "tab_cs": cs, "tab_w1": w1, "tab_e2p": e2p, "tab_id": np.eye(128, dtype=np.float32),
    }
    ppbase = np.zeros((128, NPP), np.float32)

    def put(name, arr):
        ppbase[:, PP[name]:PP[name] + arr.shape[1]] = arr
    put("ada_b", np.concatenate([_vec(np.asarray(ada_b)[i], 48) for i in range(4)], 1))
    put("nmix", np.concatenate([_vec(np.asarray(norm_mix)[i], 8) for i in range(4)], 1))
    put("nmlp", np.concatenate([_vec(np.asarray(norm_mlp)[i], 8) for i in range(4)], 1))
    put("fin", _vec(final_norm, 8))
    put("aconv", np.concatenate([_vec(np.asarray(a_conv_w)[j, t], 8) for j in range(2) for t in range(3)], 1))
    put("bb", _vec(np.asarray(b_b_out)[0], 8))
    put("cb1", _vec(np.asarray(c_b_pw1)[0], 16))
    put("cdw", np.concatenate([_vec(np.asarray(c_dw_w)[0, t], 8) for t in range(31)], 1))
    put("cdb", _vec(np.asarray(c_dw_b)[0], 8))
    put("lng", _vec(np.asarray(c_ln_g)[0], 8))
    put("lnb", _vec(np.asarray(c_ln_b)[0], 8))
    put("cb2", _vec(np.asarray(c_b_pw2)[0], 8))
    ppbase[:, PP["eps"]] = EPS

    in_maps = []
    for j in range(8):
        own = [2 * j, 2 * j + 1]
        samp_order = own + [g for g in range(16) if g not in own]
        xsj = np.zeros((NSEGA, D, W), np.float32)
        mask = np.zeros((NSEGA, 2 * H), np.float32)
        for seg in range(NSEGA):
            if 2 <= seg < 6:
                src, L, t0 = x_prompt[j], SP, T * (seg - 2) - H
            else:
                g = samp_order[seg if seg < 2 else seg - 4]
                src, L, t0 = x_sample[0], SS, T * g - H
            lo, hi = max(t0, 0), min(t0 + W, L)
            xsj[seg][:, lo - t0:hi - t0] = src[lo:hi].T
            tl = t0 + np.arange(H)
            tr = t0 + W - H + np.arange(H)
            mask[seg, :H] = ((tl >= 0) & (tl < L))
            mask[seg, H:] = ((tr >= 0) & (tr < L))
        ppj = ppbase.copy()
        ppj[:, PP["mask"]:PP["mask"] + NSEGA * 2 * H] = mask.reshape(1, -1)
        cj = np.stack([np.asarray(c_prompt, np.float32)[j], np.asarray(c_sample, np.float32)[0]], 1)
        ppj[:, PP["c"]:PP["c"] + 16] = cj.reshape(8, 128, 2).transpose(1, 0, 2).reshape(128, 16)
        a_loc = np.arange(128)
        a_glob = 8 * np.asarray(samp_order)[a_loc // 8] + (a_loc % 8)
        w1s = np.ascontiguousarray(w1[a_glob, :])
        b = np.arange(128)[:, None, None]
        kk = 128 * (16 * j - 1 + np.arange(NKK))[None, None, :] + np.arange(128)[None, :, None]
        ang = 2 * np.pi * b * kk / SS
        valid = ((kk >= 0) & (kk < SS)).astype(np.float64) / np.sqrt(SS)
        e2w = np.concatenate([np.cos(ang) * valid, -np.sin(ang) * valid], 2).astype(np.float32)
        d = dict(shared)
        d.update({"xs": xsj, "pp": ppj, "tab_w1s": w1s, "tab_e2w": e2w})
        in_maps.append(d)
    res = run_bass_kernel_spmd(nc, in_maps, core_ids=list(range(8)))
    y_prompt = np.zeros((8, SP, D), np.float32)
    y_sample = np.zeros((1, SS, D), np.float32)
    for j in range(8):
        y = res.results[j]["y"]
        y_sample[0, 2048 * j:2048 * j + 2048] = y[:, 0:2048].T
        y_prompt[j] = y[:, 2048:].T
    return (y_prompt, y_sample)
```
